# Optimizing a Trainium2 kernel written in Bass

```python
import jax, jax.numpy as jnp
from jax import lax
import numpy as np

D_MODEL = 1024
BATCH = 16
SEQ = 2048
DEPTH = 1

EPS = 1e-6
POOL_WINDOWS = (2, 4, 8, 16)
N_POOL_GROUPS = len(POOL_WINDOWS)
POOL_WIDTH = D_MODEL // 2
POOL_GC = POOL_WIDTH // N_POOL_GROUPS
HEAD_DIM = 64
N_Q_HEADS = (D_MODEL // 2) // HEAD_DIM
N_KV_HEADS = 2
GROUP = N_Q_HEADS // N_KV_HEADS
ATTN_WIDTH = N_Q_HEADS * HEAD_DIM
KV_WIDTH = N_KV_HEADS * HEAD_DIM
WINDOW = 128
BLOCK = 128
NEG_INF = -1e30
ROPE_THETA = 500000.0
ROT_DIM = HEAD_DIM // 4
N_BRANCHES = 2
GATE_WIDTH = N_BRANCHES * D_MODEL
IN_SPLITS = (POOL_WIDTH, POOL_WIDTH + ATTN_WIDTH, POOL_WIDTH + ATTN_WIDTH + KV_WIDTH,
             POOL_WIDTH + ATTN_WIDTH + 2 * KV_WIDTH)
IN_WIDTH = POOL_WIDTH + ATTN_WIDTH + 2 * KV_WIDTH + GATE_WIDTH
D_FF = 4 * D_MODEL

kernel_name = "hybrid_pool_swa_sink_gated_block"


def _rmsnorm(x, g):
    xf = x.astype(jnp.float32)
    r = lax.rsqrt(jnp.mean(xf * xf, axis=-1, keepdims=True) + EPS)
    return (xf * r * g.astype(jnp.float32)).astype(x.dtype)


def _partial_rotary(t, cos, sin):
    half = ROT_DIM // 2
    t1 = t[..., :half]
    t2 = t[..., half:ROT_DIM]
    c = cos[None, :, None, :].astype(t.dtype)
    s = sin[None, :, None, :].astype(t.dtype)
    return jnp.concatenate([t1 * c - t2 * s, t2 * c + t1 * s, t[..., ROT_DIM:]], axis=-1)


def _multiscale_pool(u, w_pool, pool_scale):
    B, S, _ = u.shape
    uf = u.astype(jnp.float32)
    cs = jnp.pad(jnp.cumsum(uf, axis=1), ((0, 0), (1, 0), (0, 0)))
    t = jnp.arange(S)
    pooled = []
    for gi, w in enumerate(POOL_WINDOWS):
        c = cs[..., gi * POOL_GC:(gi + 1) * POOL_GC]
        upper = c[:, 1:]
        lower = jnp.pad(c[:, :S + 1 - w], ((0, 0), (w - 1, 0), (0, 0)))
        count = jnp.minimum(t + 1, w).astype(jnp.float32)[None, :, None]
        pooled.append((upper - lower) / count)
    pooled = jnp.stack(pooled, axis=2)
    diff = (pooled - uf.reshape(B, S, N_POOL_GROUPS, POOL_GC)).astype(u.dtype)
    mixed = jnp.einsum('bsgc,gcd->bsgd', diff, w_pool)
    return mixed.reshape(B, S, POOL_WIDTH) * pool_scale


def _sliding_window_sink_attention(q, k, v, sinks):
    B, S = q.shape[0], q.shape[1]
    nb = S // BLOCK
    qb = q.reshape(B, nb, BLOCK, N_KV_HEADS, GROUP, HEAD_DIM)

    def with_prev(t):
        tb = t.reshape(B, nb, BLOCK, N_KV_HEADS, HEAD_DIM)
        prev = jnp.pad(tb[:, :-1], ((0, 0), (1, 0), (0, 0), (0, 0), (0, 0)))
        return jnp.concatenate([prev, tb], axis=2)

    kk = with_prev(k)
    vv = with_prev(v)
    scale = HEAD_DIM ** -0.5
    s = jnp.einsum('bnqhgd,bnkhd->bnhgqk', qb, kk).astype(jnp.float32) * scale
    qi = jnp.arange(BLOCK)[:, None]
    kj = jnp.arange(2 * BLOCK)[None, :]
    rel = qi + BLOCK - kj
    band = (rel >= 0) & (rel < WINDOW)
    has_prev = (jnp.arange(nb) > 0)[:, None, None] | (kj >= BLOCK)[None]
    valid = band[None] & has_prev
    s = jnp.where(valid[None, :, None, None], s, NEG_INF)
    sink = jnp.broadcast_to(sinks.astype(jnp.float32).reshape(1, 1, N_KV_HEADS, GROUP, 1, 1),
                            s.shape[:-1] + (1,))
    p = jax.nn.softmax(jnp.concatenate([s, sink], axis=-1), axis=-1)[..., :-1]
    o = jnp.einsum('bnhgqk,bnkhd->bnqhgd', p.astype(v.dtype), vv)
    return o.reshape(B, S, ATTN_WIDTH)


def setup_inputs(seed: int = 0) -> dict:
    key = jax.random.key(seed)
    ks = jax.random.split(key, 16)
    nrm = jax.random.normal
    f32 = jnp.float32

    def gain(k):
        return 1.0 + 0.1 * nrm(k, (DEPTH, D_MODEL), f32)

    return {
        "x": nrm(ks[0], (BATCH, SEQ, D_MODEL), f32),
        "g_mix_pre": gain(ks[1]),
        "w_in": nrm(ks[2], (DEPTH, D_MODEL, IN_WIDTH), f32) * D_MODEL ** -0.5,
        "b_in": 0.02 * nrm(ks[3], (DEPTH, IN_WIDTH), f32),
        "w_pool": nrm(ks[4], (DEPTH, N_POOL_GROUPS, POOL_GC, POOL_GC), f32) * POOL_GC ** -0.5,
        "pool_scale": 1.0 + 0.1 * nrm(ks[5], (DEPTH, POOL_WIDTH), f32),
        "attn_sinks": 0.5 * nrm(ks[6], (DEPTH, N_Q_HEADS), f32),
        "w_branch_pool": nrm(ks[7], (DEPTH, POOL_WIDTH, D_MODEL), f32) * POOL_WIDTH ** -0.5,
        "w_branch_attn": nrm(ks[8], (DEPTH, ATTN_WIDTH, D_MODEL), f32) * ATTN_WIDTH ** -0.5,
        "w_out": nrm(ks[9], (DEPTH, D_MODEL, D_MODEL), f32) * D_MODEL ** -0.5,
        "g_mix_post": gain(ks[10]),
        "g_mlp_pre": gain(ks[11]),
        "w_up": nrm(ks[12], (DEPTH, D_MODEL, D_FF), f32) * D_MODEL ** -0.5,
        "w_down": nrm(ks[13], (DEPTH, D_FF, D_MODEL), f32) * D_FF ** -0.5,
        "g_mlp_post": gain(ks[14]),
    }


def reference(x, g_mix_pre, w_in, b_in, w_pool, pool_scale, attn_sinks, w_branch_pool,
              w_branch_attn, w_out, g_mix_post, g_mlp_pre, w_up, w_down, g_mlp_post):
    B, S, _ = x.shape
    pos = jnp.arange(S, dtype=jnp.float32)
    inv_freq = ROPE_THETA ** (-jnp.arange(0, ROT_DIM, 2, dtype=jnp.float32) / ROT_DIM)
    ang = pos[:, None] * inv_freq[None, :]
    cos, sin = jnp.cos(ang), jnp.sin(ang)

    for l in range(DEPTH):
        h = _rmsnorm(x, g_mix_pre[l])
        proj = jnp.einsum('bsd,de->bse', h, w_in[l]) + b_in[l]
        u_pool, q, k, v, gates = jnp.split(proj, IN_SPLITS, axis=-1)

        y_pool = _multiscale_pool(u_pool, w_pool[l], pool_scale[l])

        q = _partial_rotary(q.reshape(B, S, N_Q_HEADS, HEAD_DIM), cos, sin)
        k = _partial_rotary(k.reshape(B, S, N_KV_HEADS, HEAD_DIM), cos, sin)
        v = v.reshape(B, S, N_KV_HEADS, HEAD_DIM)
        y_attn = _sliding_window_sink_attention(
            q.reshape(B, S, N_KV_HEADS, GROUP, HEAD_DIM), k, v, attn_sinks[l])

        g = jax.nn.sigmoid(gates.astype(jnp.float32)).astype(x.dtype)
        g_pool, g_attn = g[..., :D_MODEL], g[..., D_MODEL:]
        merged = (g_pool * jnp.einsum('bsc,cd->bsd', y_pool, w_branch_pool[l])
                  + g_attn * jnp.einsum('bsc,cd->bsd', y_attn, w_branch_attn[l]))
        mix = jnp.einsum('bsd,de->bse', merged, w_out[l])
        x = x + _rmsnorm(mix, g_mix_post[l])

        h2 = _rmsnorm(x, g_mlp_pre[l])
        ff = jnp.einsum('bsf,fd->bsd',
                        jnp.square(jax.nn.relu(jnp.einsum('bsd,df->bsf', h2, w_up[l]))), w_down[l])
        x = x + _rmsnorm(ff, g_mlp_post[l])
    return x
```

```python
import contextlib
import os
KNOB = os.environ.get('KNOB', '')
import numpy as np
import concourse.bass as bass
import concourse.mybir as mybir
from concourse.bass_utils import run_bass_kernel_spmd

F32 = mybir.dt.float32
BF16 = mybir.dt.bfloat16
I32 = mybir.dt.int32
AF = mybir.ActivationFunctionType
ALU = mybir.AluOpType

NCORES = 8
D = 1024
SEQ = 2048
INW = 3328
DFF = 4096
TOK_PER_CORE = 2 * SEQ
T = 512
NTILES = TOK_PER_CORE // T
R = 18
EPS = 1e-6
SAFE_DIST = 1 << 30
ROPE_THETA = 500000.0
MAGIC = 12582912.0


class Op:
    __slots__ = ("eng", "fn", "deps", "idx", "signal", "sigval", "semkey", "clock", "is_dma", "ninc")


class Sched:
    def __init__(self, nc, engines, sems):
        self.nc = nc
        self.engines = engines
        self.sems = sems
        self.ops = []
        self.cnt = {k: 0 for k in engines}
        self.last_writer = {}
        self.readers = {}
        self.uid = 0

    def add(self, eng, fn, reads=(), writes=(), dma_sem=None, ninc=1):
        op = Op()
        op.eng = eng
        op.fn = fn
        op.idx = self.cnt[eng]
        self.cnt[eng] += 1
        op.signal = False
        op.is_dma = dma_sem is not None
        op.semkey = dma_sem if op.is_dma else eng
        op.ninc = ninc
        op.sigval = None
        op.clock = None
        deps = {}
        for r in list(reads) + list(writes):
            w = self.last_writer.get(r)
            if w is not None:
                deps[id(w)] = w
        for r in writes:
            for o in self.readers.get(r, {}).values():
                deps[id(o)] = o
        final = []
        for d in deps.values():
            if d is op:
                continue
            if (not d.is_dma) and (not op.is_dma) and d.eng == eng:
                if eng == "pe":
                    continue
                if op.idx - d.idx >= SAFE_DIST:
                    continue
            d.signal = True
            final.append(d)
        op.deps = final
        for r in writes:
            self.last_writer[r] = op
            self.readers[r] = {}
        for r in reads:
            key = eng
            if op.is_dma:
                self.uid += 1
                key = ("dma", self.uid)
            self.readers.setdefault(r, {})[key] = op
        self.ops.append(op)
        return op

    def emit(self):
        counters = {}
        for op in self.ops:
            if op.is_dma:
                counters[op.semkey] = counters.get(op.semkey, 0) + 16 * op.ninc
                op.sigval = counters[op.semkey]
            elif op.signal:
                counters[op.semkey] = counters.get(op.semkey, 0) + 1
                op.sigval = counters[op.semkey]
        clocks = {k: {} for k in self.engines}
        for op in self.ops:
            E = self.engines[op.eng]
            clk = clocks[op.eng]
            for d in op.deps:
                if clk.get(d.semkey, 0) >= d.sigval:
                    continue
                E.wait_ge(self.sems[d.semkey], d.sigval)
                if d.clock:
                    for k, v in d.clock.items():
                        if clk.get(k, 0) < v:
                            clk[k] = v
                clk[d.semkey] = d.sigval
            res = op.fn()
            if op.is_dma:
                if not isinstance(res, (list, tuple)):
                    res = [res]
                assert len(res) == op.ninc
                for ins in res:
                    ins.then_inc(self.sems[op.semkey], 16)
                op.clock = dict(clk)
            elif op.signal:
                res.then_inc(self.sems[op.semkey], 1)
                op.clock = dict(clk)
        return counters


class _Stop(Exception):
    pass


def build_program(ntiles=NTILES, debug=False, stop=""):
    nc = bass.Bass("TRN2", target_bir_lowering=False)
    es = contextlib.ExitStack()

    def din(name, shape):
        return nc.dram_tensor(name, shape, F32, kind="ExternalInput").ap()

    x_d = din("x", [TOK_PER_CORE, D])
    g_mix_pre = din("g_mix_pre", [1, D])
    w_in = din("w_in", [D, INW])
    b_in = din("b_in", [1, INW])
    w_pool = din("w_pool", [4, 128, 128])
    pool_scale = din("pool_scale", [1, 512])
    attn_sinks = din("attn_sinks", [1, 8])
    w_bp = din("w_branch_pool", [512, D])
    w_ba = din("w_branch_attn", [512, D])
    w_out = din("w_out", [D, D])
    g_mix_post = din("g_mix_post", [1, D])
    g_mlp_pre = din("g_mlp_pre", [1, D])
    w_up = din("w_up", [D, DFF])
    w_down = din("w_down", [DFF, D])
    g_mlp_post = din("g_mlp_post", [1, D])
    y_d = nc.dram_tensor("y", [TOK_PER_CORE, D], F32, kind="ExternalOutput").ap()

    def scratch(name, shape):
        return nc.dram_tensor(name, shape, BF16, kind="Internal").ap()

    s_win = scratch("s_win", [D, INW])
    s_bp = scratch("s_bp", [512, D])
    s_ba = scratch("s_ba", [512, D])
    s_wout = scratch("s_wout", [D, D])
    s_wup = scratch("s_wup", [D, DFF])
    s_wdn = scratch("s_wdn", [DFF, D])

    def sb(name, shape, dt):
        return es.enter_context(nc.sbuf_tensor(name, shape, dt))

    def psum(name, shape, dt):
        return es.enter_context(nc.psum_tensor(name, shape, dt))

    ring = sb("ring", [128, R, 1024], BF16)
    xb = sb("xb", [128, 8, 1024], F32)
    hT = sb("hT", [128, 8, 512], BF16)
    htok = sb("htok", [128, 2, 1024], BF16)
    up = sb("up", [128, 4, 528], F32)
    ptmp = sb("ptmp", [128, 2, 528], F32)
    diffT = sb("diffT", [128, 4, 512], BF16)
    act = sb("act", [128, 32, 512], BF16)
    qkvf = sb("qkvf", [128, 2, 768], F32)
    qkb = sb("qkb", [128, 4, 640], BF16)
    ropet = sb("ropet", [128, 2, 4, 80], F32)
    qT = sb("qT", [128, 4, 512], BF16)
    kT = sb("kT", [128, 8, 128], BF16)
    vb = sb("vb", [128, 8, 128], BF16)
    pT = sb("pT", [128, 4, 1024], BF16)
    rec = sb("rec", [128, 1, 512], F32)
    t1 = sb("t1", [128, 2, 512], F32)
    t2 = sb("t2", [128, 2, 512], F32)
    rl = sb("rl", [128, 2, 512], F32)
    tmpn = rl[:, :, :].rearrange("p a b -> p (a b)")
    ffsb = sb("ffsb", [128, 4, 512], F32)
    stats = sb("stats", [128, 128], F32)
    gpre = sb("gpre", [128, 1024], F32)
    gpost = sb("gpost", [128, 1024], F32)
    gpre2 = sb("gpre2", [128, 1024], F32)
    gpost2 = sb("gpost2", [128, 1024], F32)
    bqkv = sb("bqkv", [128, 768], F32)
    bB = sb("bB", [128, 20], F32)
    pscale = sb("pscale", [128, 4], F32)
    sinkexp = sb("sinkexp", [128, 512], F32)
    sink4 = sb("sink4", [128, 4], F32)
    wpool = sb("wpool", [128, 4, 128], BF16)
    ident = sb("ident", [128, 128], BF16)
    maskneg = sb("maskneg", [128, 2, 512], BF16)
    ones = sb("ones", [128, 64], BF16)
    epst = sb("epst", [128, 1], F32)
    posi = sb("posi", [128, 16], I32)
    posf = sb("posf", [128, 16], F32)
    cost = sb("cost", [128, 16, 8], F32)
    sint = sb("sint", [128, 16, 8], F32)
    invc = sb("invc", [128, 4, 16], F32)
    invi = sb("invi", [128, 16], I32)

    identf = tmpn[:, 0:128]
    maskf = tmpn[:, 128:384].rearrange("p (a b) -> p a b", a=2)
    ang = tmpn[:, 384:512].rearrange("p (a b) -> p a b", a=16)
    angk = tmpn[:, 512:640].rearrange("p (a b) -> p a b", a=16)
    angt = tmpn[:, 640:768].rearrange("p (a b) -> p a b", a=16)

    tr = [psum("tr%d" % i, [128, 8, 128], BF16) for i in range(2)]
    pm = [psum("pm%d" % i, [128, 1024], F32) for i in range(3)]

    sems = {}

    def mksem(name):
        sems[name] = es.enter_context(nc.semaphore(name))

    for e in ("pe", "act", "dve", "pool", "pre", "cst"):
        mksem(e)
    for s in range(R):
        mksem("ring%d" % s)
    for g in range(13):
        mksem("cast%d" % g)
    for s in range(R):
        mksem("wb%d" % s)
        mksem("rq%d" % s)
    for b in range(8):
        mksem("xl%d" % b)
        mksem("xs%d" % b)

    engines = {"pe": nc.tensor, "act": nc.scalar, "dve": nc.vector, "pool": nc.gpsimd, "sp": nc.sync}
    S = Sched(nc, engines, sems)
    VE = {"dve": nc.vector, "pool": nc.gpsimd}

    def ckpt(name):
        if stop == name:
            raise _Stop()

    def mm(out, lhsT, rhs, start, stop, reads, writes, tp=None):
        def fn():
            if tp is None:
                return nc.tensor.matmul(out, lhsT=lhsT, rhs=rhs, start=start, stop=stop)
            return nc.tensor.matmul(out, lhsT=lhsT, rhs=rhs, start=start, stop=stop, tile_position=tp)
        S.add("pe", fn, reads, writes)

    def transpose(out, in_, reads, writes):
        S.add("pe", lambda: nc.tensor.transpose(out, in_, ident[:]), list(reads) + ["ident"], writes)

    def actf(out, in_, func, reads, writes, bias=None, scale=None, accum=None):
        def fn():
            kw = {}
            if bias is not None:
                kw["bias"] = bias
            if scale is not None:
                kw["scale"] = scale
            if accum is not None:
                kw["accum_out"] = accum
            return nc.scalar.activation(out=out, in_=in_, func=func, **kw)
        S.add("act", fn, reads, writes)

    def tt(eng, out, in0, in1, op, reads, writes):
        S.add(eng, lambda: VE[eng].tensor_tensor(out=out, in0=in0, in1=in1, op=op), reads, writes)

    def ts(eng, out, in0, s1, s2, op0, op1, reads, writes):
        def fn():
            if op1 is None:
                return VE[eng].tensor_scalar(out=out, in0=in0, scalar1=s1, scalar2=None, op0=op0)
            return VE[eng].tensor_scalar(out=out, in0=in0, scalar1=s1, scalar2=s2, op0=op0, op1=op1)
        S.add(eng, fn, reads, writes)

    def stt(eng, out, in0, scalar, in1, op0, op1, reads, writes):
        S.add(eng, lambda: VE[eng].scalar_tensor_tensor(out=out, in0=in0, scalar=scalar, in1=in1, op0=op0, op1=op1),
              reads, writes)

    def copy(eng, out, in_, reads, writes):
        if eng == "act":
            S.add("act", lambda: nc.scalar.copy(out=out, in_=in_), reads, writes)
        else:
            S.add(eng, lambda: VE[eng].tensor_copy(out=out, in_=in_), reads, writes)

    def recip(out, in_, reads, writes):
        S.add("dve", lambda: nc.vector.reciprocal(out=out, in_=in_), reads, writes)

    def memset(eng, out, val, writes):
        S.add(eng, lambda: VE[eng].memset(out, val), [], writes)

    def dma(eng, out, in_, semkey, reads, writes):
        q = {"sp": nc.sync, "pool": nc.gpsimd}[eng]
        S.add(eng, lambda: q.dma_start(out=out, in_=in_), reads, writes, dma_sem=semkey)

    stat_ctr = [0]

    def newstat():
        c = stat_ctr[0] % 128
        stat_ctr[0] += 1
        return stats[:, c:c + 1], ("st", c)

    cast_groups = {
        0: [(s_win[:, 512:1280], w_in[:, 512:1280])],
        1: [(s_win[:, 0:512], w_in[:, 0:512])],
        2: [(s_win[:, 1280:2304], w_in[:, 1280:2304])],
        3: [(s_win[:, 2304:3328], w_in[:, 2304:3328])],
        4: [(s_bp[:, :], w_bp[:, :])],
        5: [(s_ba[:, :], w_ba[:, :])],
        6: [(s_wout[:, :], w_out[:, :])],
    }
    for cb in range(4):
        cast_groups[7 + cb] = [(s_wup[:, cb * 1024:(cb + 1) * 1024], w_up[:, cb * 1024:(cb + 1) * 1024])]
    for nh in range(2):
        cast_groups[11 + nh] = [(s_wdn[r0:r0 + 1024, nh * 512:(nh + 1) * 512], w_down[r0:r0 + 1024, nh * 512:(nh + 1) * 512])
                                for r0 in range(0, DFF, 1024)]
    NGROUPS = 13

    def issue_casts(groups, after=()):
        for n_, g in enumerate(groups):
            parts = cast_groups[g]

            def fn(parts=parts):
                return [nc.gpsimd.dma_start(out=d_, in_=s_) for (d_, s_) in parts]
            S.add("pool", fn, list(after) if n_ == 0 else [], [("scrg", g)], dma_sem="cast%d" % g, ninc=len(parts))

    pre_ops = []
    op = S.add("pool", lambda: nc.gpsimd.dma_start(out=wpool[:], in_=w_pool.rearrange("g c d -> c g d")),
               [], ["wpool"], dma_sem="pre")
    pre_ops.append(op)

    cst_ops = []

    def cdma(out, in_, w, nonc=False):
        def fn():
            if nonc:
                with nc.allow_non_contiguous_dma(reason="tiny constant load"):
                    return nc.sync.dma_start(out=out, in_=in_)
            return nc.sync.dma_start(out=out, in_=in_)
        cst_ops.append(S.add("sp", fn, [], [w], dma_sem="cst"))

    cdma(gpre[:], g_mix_pre.partition_broadcast(128), "gpre")
    cdma(gpost[:], g_mix_post.partition_broadcast(128), "gpost")
    cdma(gpre2[:], g_mlp_pre.partition_broadcast(128), "gpre2")
    cdma(gpost2[:], g_mlp_post.partition_broadcast(128), "gpost2")
    cdma(bqkv[:], b_in[:, 512:1280].partition_broadcast(128), "bqkv")
    cdma(bB[:, 0:4], b_in[0, 0:512].rearrange("(c p) -> p c", p=128), "bB", nonc=True)
    cdma(bB[:, 4:20], b_in[0, 1280:3328].rearrange("(c p) -> p c", p=128), "bB2", nonc=True)
    cdma(pscale[:], pool_scale[0, :].rearrange("(g p) -> p g", p=128), "pscale", nonc=True)
    cdma(sink4[0:64, :], attn_sinks[:, 0:4].partition_broadcast(64), "sink4a")
    cdma(sink4[64:128, :], attn_sinks[:, 4:8].partition_broadcast(64), "sink4b")

    memset("pool", identf[:], 1.0, ["identf"])
    S.add("pool", lambda: nc.gpsimd.affine_select(out=identf[:], in_=identf[:], pattern=[[-1, 128]],
                                                  compare_op=ALU.is_equal, fill=0.0, base=0, channel_multiplier=1),
          ["identf"], ["identf"])
    copy("dve", ident[:], identf[:], ["identf"], ["ident"])
    memset("pool", maskf[:], 1.0, ["maskf"])
    S.add("pool", lambda: nc.gpsimd.affine_select(out=maskf[:, 0, :], in_=maskf[:, 0, :], pattern=[[-1, 128]],
                                                  compare_op=ALU.is_gt, fill=0.0, base=0, channel_multiplier=1),
          ["maskf"], ["maskf"])
    S.add("pool", lambda: nc.gpsimd.affine_select(out=maskf[:, 1, :], in_=maskf[:, 1, :], pattern=[[1, 128]],
                                                  compare_op=ALU.is_ge, fill=0.0, base=0, channel_multiplier=-1),
          ["maskf"], ["maskf"])
    for kb_ in range(2):
        ts("dve", maskneg[:, kb_, :].rearrange("p (g q) -> p g q", g=4),
           maskf[:, kb_, :].unsqueeze(1).broadcast_to([128, 4, 128]), -1.0, 30000.0, ALU.add, ALU.mult,
           ["maskf"], ["maskneg"])
    memset("dve", ones[:], 1.0, ["ones"])
    memset("dve", epst[:], EPS, ["eps"])
    actf(sink4[:], sink4[:], AF.Exp, ["sink4a", "sink4b"], ["sink4"])
    copy("dve", sinkexp[:].rearrange("p (g q) -> p g q", g=4), sink4[:].unsqueeze(2).broadcast_to([128, 4, 128]),
         ["sink4"], ["sinkexp"])
    S.add("pool", lambda: nc.gpsimd.iota(posi[:], pattern=[[128, 16]], base=0, channel_multiplier=1), [], ["posi"])
    copy("dve", posf[:], posi[:], ["posi"], ["posf"])
    for i in range(8):
        inv_freq = float(np.float32(ROPE_THETA) ** np.float32(-(2.0 * i) / 16.0))
        ts("dve", ang[:, :, i], posf[:], inv_freq, None, ALU.mult, None, ["posf"], ["ang"])
    C1 = 6.28125
    C2 = float(2.0 * np.pi - 6.28125)
    for (tab, shift, nm) in ((sint, 0.0, "sint"), (cost, float(np.pi / 2), "cost")):
        ts("dve", angt[:], ang[:], shift, None, ALU.add, None, ["ang"], ["angt"])
        ts("dve", angk[:], angt[:], float(1.0 / (2.0 * np.pi)), MAGIC, ALU.mult, ALU.add, ["angt"], ["angk"])
        ts("dve", angk[:], angk[:], -MAGIC, None, ALU.add, None, ["angk"], ["angk"])
        stt("dve", angt[:], angk[:], -C1, angt[:], ALU.mult, ALU.add, ["angk", "angt"], ["angt"])
        stt("dve", angt[:], angk[:], -C2, angt[:], ALU.mult, ALU.add, ["angk", "angt"], ["angt"])
        ts("dve", angt[:], angt[:], 3.1415925, -3.1415925, ALU.min, ALU.max, ["angt"], ["angt"])
        actf(tab[:], angt[:], AF.Sin, ["angt"], [nm])
    S.add("pool", lambda: nc.gpsimd.iota(invi[:], pattern=[[1, 16]], base=1, channel_multiplier=0), [], ["invi"])
    for gi in range(4):
        copy("dve", invc[:, gi, :], invi[:], ["invi"], ["invc"])
    for gi in range(4):
        ts("dve", invc[:, gi, :], invc[:, gi, :], float(2 ** (gi + 1)), None, ALU.min, None, ["invc"], ["invc"])
    recip(invc[:], invc[:], ["invc"], ["invc"])

    CONST_R = ["gpre", "gpost", "gpre2", "gpost2", "bqkv", "bB", "bB2", "pscale", "sinkexp", "wpool", "ident",
               "maskneg", "ones", "eps", "sint", "cost", "invc"]

    pieces = []

    def both(slicer, s_t, w_t):
        return slicer(s_t), slicer(w_t)

    def mixer_parts():
        out = []
        for k in range(8):
            out.append([(lambda s: ring[:, s, 0:768],) + both(lambda t, k=k: t[k * 128:(k + 1) * 128, 512:1280], s_win, w_in)])
        for k in range(8):
            out.append([(lambda s: ring[:, s, 0:512],) + both(lambda t, k=k: t[k * 128:(k + 1) * 128, 0:512], s_win, w_in)])
        for cb in range(2):
            for k in range(8):
                out.append([(lambda s: ring[:, s, :],) + both(
                    lambda t, k=k, cb=cb: t[k * 128:(k + 1) * 128, 1280 + cb * 1024:1280 + (cb + 1) * 1024], s_win, w_in)])
        for c in range(4):
            out.append([(lambda s: ring[:, s, :],) + both(lambda t, c=c: t[c * 128:(c + 1) * 128, :], s_bp, w_bp)])
        for g in range(4):
            out.append([(lambda s: ring[0:64, s, :],) + both(lambda t, g=g: t[g * 64:(g + 1) * 64, :], s_ba, w_ba),
                        (lambda s: ring[64:128, s, :],) + both(lambda t, g=g: t[(4 + g) * 64:(5 + g) * 64, :], s_ba, w_ba)])
        for k in range(8):
            out.append([(lambda s: ring[:, s, :],) + both(lambda t, k=k: t[k * 128:(k + 1) * 128, :], s_wout, w_out)])
        return out

    def mlp_parts():
        out = []
        for cb in range(4):
            for k in range(8):
                out.append((7 + cb, [(lambda s: ring[:, s, :],) + both(
                    lambda t, k=k, cb=cb: t[k * 128:(k + 1) * 128, cb * 1024:(cb + 1) * 1024], s_wup, w_up)]))
        for nh in range(2):
            for jp in range(16):
                out.append((11 + nh, [(lambda s: ring[:, s, :].rearrange("p (two n) -> p two n", two=2),) + both(
                    lambda t, jp=jp, nh=nh: t[jp * 256:(jp + 1) * 256, nh * 512:(nh + 1) * 512].rearrange("(two p) n -> p two n", p=128),
                    s_wdn, w_down)]))
        return out

    def M(first):
        return [("sw", ("scr", j), p) if first else ("hw", ("scr", j), p) for j, p in enumerate(mixer_parts())]

    def L():
        return [("hw", ("scrg", g), p) for (g, p) in mlp_parts()]

    if ntiles >= 2:
        pieces += M(True) + M(False) + L() + L()
        for i in range(2, ntiles):
            pieces += M(False) + L()
    else:
        pieces += M(True) + L()
    rstate = {"loaded": 0, "cp": 0}

    def ring_prefetch(upto):
        upto = min(upto, len(pieces) - 1)
        while rstate["loaded"] <= upto:
            m = rstate["loaded"]
            s = m % R
            mode, res, parts = pieces[m]
            if mode == "sw":
                def fn(parts=parts, s=s):
                    return [nc.gpsimd.dma_start(out=dst(s), in_=w32) for (dst, sc, w32) in parts]
                S.add("pool", fn, [], [("ring", s)], dma_sem="rq%d" % s, ninc=len(parts))

                def fnw(parts=parts, s=s):
                    return [nc.sync.dma_start(out=sc, in_=dst(s)) for (dst, sc, w32) in parts]
                S.add("sp", fnw, [("ring", s)], [res], dma_sem="wb%d" % s, ninc=len(parts))
            else:
                def fn(parts=parts, s=s):
                    return [nc.sync.dma_start(out=dst(s), in_=sc) for (dst, sc, w32) in parts]
                S.add("sp", fn, [res], [("ring", s)], dma_sem="ring%d" % s, ninc=len(parts))
            rstate["loaded"] += 1

    def ring_acquire(n):
        cp = rstate["cp"]
        ring_prefetch(cp + R - 1)
        assert rstate["loaded"] >= cp + n
        return [(cp + i) % R for i in range(n)]

    def ring_release(n):
        rstate["cp"] += n
        ring_prefetch(rstate["cp"] + R - 1)

    deferred = []

    def flush_deferred():
        while deferred:
            deferred.pop(0)()

    def load_x(i):
        flush_deferred()
        xbuf = i % 2
        for tb in range(4):
            b = xbuf * 4 + tb
            r0 = i * T + tb * 128
            dma("sp", xb[:, b, :], x_d[r0:r0 + 128, :], "xl%d" % b, [], [("x", b)])

    def store_y(i, tb):
        b = (i % 2) * 4 + tb
        r0 = i * T + tb * 128
        dma("sp", y_d[r0:r0 + 128, :], xb[:, b, :], "xs%d" % b, [("x", b)], [("y", i, tb)])

    bank_ctr = [0]

    def next_half():
        h = bank_ctr[0] % 6
        bank_ctr[0] += 1
        return pm[h // 2][:, (h % 2) * 512:(h % 2 + 1) * 512], ("ps", h)

    full_ctr = [0]

    def next_full():
        f = full_ctr[0] % 3
        full_ctr[0] += 1
        return f

    tr_ctr = [0]

    def next_tr():
        t_ = tr_ctr[0] % 2
        tr_ctr[0] += 1
        return tr[t_], ("tr", t_)

    def rms_scale(ss_ap, ss_res):
        rs, rs_res = newstat()
        actf(rs, ss_ap, AF.Sqrt, [ss_res, "eps"], [rs_res], bias=epst[:], scale=1.0 / D)
        r, r_res = newstat()
        recip(r, rs, [rs_res], [r_res])
        return r, r_res

    def norm_transpose(b, tb, gain, gain_res):
        hb = b % 2
        ss, ss_res = newstat()
        actf(htok[:, hb, :], xb[:, b, :], AF.Square, [("x", b)], [ss_res, ("htok", hb)], accum=ss)
        r, r_res = rms_scale(ss, ss_res)
        stt("dve", htok[:, hb, :], xb[:, b, :], r, gain[:], ALU.mult, ALU.mult,
            [("x", b), r_res, gain_res], [("htok", hb)])
        trt, tr_res = next_tr()
        for k in range(8):
            transpose(trt[:, k, :], htok[:, hb, k * 128:(k + 1) * 128], [("htok", hb)], [tr_res])
        copy("act", hT[:, :, tb * 128:(tb + 1) * 128], trt[:, :, :], [tr_res], [("hT", tb)])

    def front(i):
        for tb in range(4):
            b = (i % 2) * 4 + tb
            norm_transpose(b, tb, gpre, "gpre")

    HT_ALL = [("hT", tb) for tb in range(4)]

    def norm_part(b, gain, gain_res):
        hb = b % 2
        ss, ss_res = newstat()
        actf(htok[:, hb, :], xb[:, b, :], AF.Square, [("x", b)], [ss_res, ("htok", hb)], accum=ss)
        r, r_res = rms_scale(ss, ss_res)
        stt("dve", htok[:, hb, :], xb[:, b, :], r, gain[:], ALU.mult, ALU.mult,
            [("x", b), r_res, gain_res], [("htok", hb)])

    def transpose_part(b, tb):
        hb = b % 2
        trt, tr_res = next_tr()
        for k in range(8):
            transpose(trt[:, k, :], htok[:, hb, k * 128:(k + 1) * 128], [("htok", hb)], [tr_res])
        copy("act", hT[:, :, tb * 128:(tb + 1) * 128], trt[:, :, :], [tr_res], [("hT", tb)])

    def gate_block(cb):
        sl = ring_acquire(8)
        for cc in range(8):
            c = cb * 8 + cc
            ps_ap, ps_res = next_half()
            for k in range(8):
                mm(ps_ap, ring[:, sl[k], cc * 128:(cc + 1) * 128], hT[:, k, :], k == 0, k == 7,
                   HT_ALL + [("ring", sl[k])], [ps_res])
            actf(act[:, c, :], ps_ap, AF.Sigmoid, [ps_res, "bB2"], [("act", c)], bias=bB[:, 4 + c:5 + c])
        ring_release(8)

    def mixer(i, gen="own"):
        xbuf = i % 2
        if gen == "own":
            gb0, ggain, gres = xbuf * 4, gpre2, "gpre2"
        else:
            gb0, ggain, gres = gen
        ti = i % 4
        sl = ring_acquire(8)
        for tb in range(4):
            f = next_full()
            for k in range(8):
                lhsT = hT[:, k, tb * 128:(tb + 1) * 128]
                mm(pm[f][:, 0:512], lhsT, ring[:, sl[k], 0:512], k == 0, k == 7,
                   [("hT", tb), ("ring", sl[k])], [("ps", 2 * f)])
                mm(pm[f][:, 512:768], lhsT, ring[:, sl[k], 512:768], k == 0, k == 7,
                   [("hT", tb), ("ring", sl[k])], [("ps", 2 * f + 1)])
            qb = tb % 2
            tt("dve", qkvf[:, qb, 0:512].rearrange("p (g kv d) -> p kv g d", g=4, kv=2),
               pm[f][:, 0:512].rearrange("p (kv g d) -> p kv g d", kv=2, g=4),
               bqkv[:, 0:512].rearrange("p (kv g d) -> p kv g d", kv=2, g=4), ALU.add,
               [("ps", 2 * f), "bqkv"], [("qkvf", qb, 0)])
            tt("dve", qkvf[:, qb, 512:768], pm[f][:, 512:768], bqkv[:, 512:768], ALU.add,
               [("ps", 2 * f + 1), "bqkv"], [("qkvf", qb, 1)])
            gb = ti * 4 + tb
            qk = qkvf[:, qb, 0:640].rearrange("p (h d) -> p h d", h=10)
            ob = qkb[:, tb, :].rearrange("p (h d) -> p h d", h=10)
            cb_ = cost[:, gb, :].unsqueeze(1).broadcast_to([128, 10, 8])
            sb_ = sint[:, gb, :].unsqueeze(1).broadcast_to([128, 10, 8])
            tA = ropet[:, qb, 0, :].rearrange("p (h d) -> p h d", h=10)
            tB = ropet[:, qb, 1, :].rearrange("p (h d) -> p h d", h=10)
            tC = ropet[:, qb, 2, :].rearrange("p (h d) -> p h d", h=10)
            tD = ropet[:, qb, 3, :].rearrange("p (h d) -> p h d", h=10)
            qres = [("qkvf", qb, 0), ("qkvf", qb, 1)]
            tt("pool", tA, qk[:, :, 0:8], cb_, ALU.mult, qres + ["cost"], [("ropet", qb, 0)])
            tt("pool", tB, qk[:, :, 8:16], sb_, ALU.mult, qres + ["sint"], [("ropet", qb, 1)])
            tt("dve", tC, qk[:, :, 8:16], cb_, ALU.mult, qres + ["cost"], [("ropet", qb, 2)])
            tt("dve", tD, qk[:, :, 0:8], sb_, ALU.mult, qres + ["sint"], [("ropet", qb, 3)])
            copy("act", ob[:, :, 16:64], qk[:, :, 16:64], qres, [("qkb", tb, 2)])
            tt("pool", ob[:, :, 0:8], tA, tB, ALU.subtract, [("ropet", qb, 0), ("ropet", qb, 1)], [("qkb", tb, 0)])
            tt("dve", ob[:, :, 8:16], tC, tD, ALU.add, [("ropet", qb, 2), ("ropet", qb, 3)], [("qkb", tb, 1)])
            slot = gb % 8
            copy("pool", vb[:, slot, :], qkvf[:, qb, 640:768], [("qkvf", qb, 1)], [("vb", slot)])
        ring_release(8)
        flush_deferred()
        ckpt("qkv")

        sl = ring_acquire(8)
        for c in range(4):
            ps_ap, ps_res = next_half()
            for k in range(8):
                mm(ps_ap, ring[:, sl[k], c * 128:(c + 1) * 128], hT[:, k, :], k == 0, k == 7,
                   HT_ALL + [("ring", sl[k])], [ps_res])
            actf(up[:, c, 16:528], ps_ap, AF.Identity, [ps_res, "bB"], [("up", c)], bias=bB[:, c:c + 1])
        ring_release(8)
        gate_block(0)

        for tb in range(4):
            gb = ti * 4 + tb
            slot = gb % 8
            trt, tr_res = next_tr()
            qkb_res = [("qkb", tb, 0), ("qkb", tb, 1), ("qkb", tb, 2)]
            for g in range(4):
                transpose(trt[:, g, :], qkb[:, tb, g * 128:(g + 1) * 128], qkb_res, [tr_res])
            transpose(trt[:, 4, :], qkb[:, tb, 512:640], qkb_res, [tr_res])
            copy("act", qT[:, tb, :].rearrange("p (g q) -> p g q", g=4), trt[:, 0:4, :], [tr_res], [("qT", tb)])
            copy("act", kT[:, slot, :], trt[:, 4, :], [tr_res], [("kT", slot)])

        if ti == 0:
            memset("pool", up[:, :, 0:16], 0.0, [("uph", g) for g in range(4)])
        for gi in range(4):
            w = 2 ** (gi + 1)
            src = up[:, gi, :]
            src_res = [("up", gi), ("uph", gi)]
            cur, cur_res = src, src_res
            sh = 1
            lo = 1
            for st_ in range(gi + 1):
                dst = ptmp[:, st_ % 2, :]
                dres = ("ptmp", st_ % 2)
                tt("dve", dst[:, lo:528], cur[:, lo:528], cur[:, lo - sh:528 - sh], ALU.add, cur_res, [dres])
                cur, cur_res = dst, [dres]
                sh *= 2
                lo += sh
            tmpc = ptmp[:, (gi + 1) % 2, 0:15]
            tres = ("ptmp", (gi + 1) % 2)
            stt("dve", diffT[:, gi, :], cur[:, 16:528], 1.0 / w, src[:, 16:528], ALU.mult, ALU.subtract,
                cur_res + src_res, [("diffT", gi)])
            if ti == 0:
                tt("dve", tmpc, cur[:, 16:31], invc[:, gi, 0:15], ALU.mult, cur_res + ["invc"], [tres])
                tt("dve", diffT[:, gi, 0:15], tmpc, src[:, 16:31], ALU.subtract, [tres] + src_res, [("diffT", gi)])
            if ti != 3:
                copy("pool", up[:, gi, 0:16], up[:, gi, 512:528], [("up", gi)], [("uph", gi)])

        gate_block(1)
        ckpt("gates")

        for gi in range(4):
            ps_ap, ps_res = next_half()
            mm(ps_ap, wpool[:, gi, :], diffT[:, gi, :], True, True, ["wpool", ("diffT", gi)], [ps_res])
            ts("dve", act[:, 28 + gi, :], ps_ap, pscale[:, gi:gi + 1], None, ALU.mult, None,
               [ps_res, "pscale"], [("act", 28 + gi)])

        ckpt("pool")
        def att_scores(tb):
            gb = ti * 4 + tb
            slot = gb % 8
            pslot = (gb - 1) % 8
            kbs = ([(0, pslot)] if gb > 0 else []) + [(1, slot)]
            for kv in range(2):
                pbuf = (tb % 2) * 2 + kv
                pv = pT[:, pbuf, :].rearrange("p (kb n) -> p kb n", kb=2)
                for (kb, ks) in kbs:
                    mm(pm[kv][:, kb * 512:(kb + 1) * 512], kT[kv * 64:(kv + 1) * 64, ks, :],
                       qT[kv * 64:(kv + 1) * 64, tb, :], True, False,
                       [("kT", ks), ("qT", tb)], [("ps", 2 * kv + kb)], tp=(kv * 64, 0))
                    mm(pm[kv][:, kb * 512:(kb + 1) * 512], ident[:, :], maskneg[:, kb, :], False, True,
                       ["ident", "maskneg"], [("ps", 2 * kv + kb)])
                    actf(pv[:, kb, :], pm[kv][:, kb * 512:(kb + 1) * 512], AF.Exp, [("ps", 2 * kv + kb)],
                         [("pT", pbuf, kb)], scale=0.125)

        def att_pv(tb):
            gb = ti * 4 + tb
            slot = gb % 8
            pslot = (gb - 1) % 8
            kbs = ([(0, pslot)] if gb > 0 else []) + [(1, slot)]
            n = len(kbs)
            if tb % 2 == 0:
                o_ps, o_res = pm[2][:, 0:512], ("ps", 4)
                d_ps, d_res = pm[2][:, 512:1024], ("ps", 5)
                rc, rc_res = rec[:, 0, :], ("rec", 0)
            else:
                o_ps, o_res = tr[0][:].rearrange("p a b -> p (a b)").bitcast(F32), ("tr", 0)
                d_ps, d_res = tr[1][:].rearrange("p a b -> p (a b)").bitcast(F32), ("tr", 1)
                rc, rc_res = ptmp[:, 0, 0:512], ("ptmp", 0)
            for kv in range(2):
                pbuf = (tb % 2) * 2 + kv
                pv = pT[:, pbuf, :].rearrange("p (kb n) -> p kb n", kb=2)
                for idx, (kb, ks) in enumerate(kbs):
                    mm(o_ps[kv * 64:(kv + 1) * 64, :], vb[:, ks, kv * 64:(kv + 1) * 64], pv[:, kb, :],
                       idx == 0, idx == n - 1, [("vb", ks), ("pT", pbuf, kb)], [o_res], tp=(0, kv * 64))
                for idx, (kb, ks) in enumerate(kbs):
                    mm(d_ps[kv * 64:(kv + 1) * 64, :], ones[:, :], pv[:, kb, :],
                       idx == 0, idx == n - 1, ["ones", ("pT", pbuf, kb)], [d_res], tp=(0, kv * 64))
            tt("dve", rc, d_ps, sinkexp[:], ALU.add, [d_res, "sinkexp"], [rc_res])
            actf(rc, rc, AF.Ln, [rc_res], [rc_res])
            actf(rc, rc, AF.Exp, [rc_res], [rc_res], scale=-1.0)
            tt("dve", act[:, 24:28, tb * 128:(tb + 1) * 128], o_ps.rearrange("p (g q) -> p g q", g=4),
               rc.rearrange("p (g q) -> p g q", g=4), ALU.mult,
               [o_res, rc_res], [("act", 24 + g) for g in range(4)])

        att_scores(0)
        for tb in range(4):
            if tb + 1 < 4:
                att_scores(tb + 1)
            att_pv(tb)

        ckpt("attn")
        sl = ring_acquire(8)
        for fo in range(8):
            f = next_full()
            for c in range(4):
                mm(pm[f][:, 0:512], ring[:, sl[c], fo * 128:(fo + 1) * 128], act[:, 28 + c, :], c == 0, c == 3,
                   [("ring", sl[c]), ("act", 28 + c)], [("ps", 2 * f)])
            for g in range(4):
                mm(pm[f][:, 512:1024], ring[:, sl[4 + g], fo * 128:(fo + 1) * 128], act[:, 24 + g, :], g == 0, g == 3,
                   [("ring", sl[4 + g]), ("act", 24 + g)], [("ps", 2 * f + 1)])
            tbuf = fo % 2
            tt("dve", t1[:, tbuf, :], pm[f][:, 0:512], act[:, fo, :], ALU.mult,
               [("ps", 2 * f), ("act", fo)], [("t1", tbuf)])
            tt("dve", t2[:, tbuf, :], pm[f][:, 512:1024], act[:, 8 + fo, :], ALU.mult,
               [("ps", 2 * f + 1), ("act", 8 + fo)], [("t2", tbuf)])
            tt("pool", act[:, 16 + fo, :], t1[:, tbuf, :], t2[:, tbuf, :], ALU.add,
               [("t1", tbuf), ("t2", tbuf)], [("act", 16 + fo)])
        ring_release(8)

        ckpt("branch")
        sl = ring_acquire(8)

        for tb in range(4):
            b = xbuf * 4 + tb
            f = next_full()
            for k in range(8):
                lhsT = act[:, 16 + k, tb * 128:(tb + 1) * 128]
                mm(pm[f][:, 0:512], lhsT, ring[:, sl[k], 0:512], k == 0, k == 7,
                   [("act", 16 + k), ("ring", sl[k])], [("ps", 2 * f)])
                mm(pm[f][:, 512:1024], lhsT, ring[:, sl[k], 512:1024], k == 0, k == 7,
                   [("act", 16 + k), ("ring", sl[k])], [("ps", 2 * f + 1)])
            ss, ss_res = newstat()
            actf(ptmp[:, :, :].rearrange("p a b -> p (a b)")[:, 0:1024], pm[f][:, :], AF.Square,
                 [("ps", 2 * f), ("ps", 2 * f + 1)], [ss_res, ("ptmp", 0), ("ptmp", 1)], accum=ss)
            r, r_res = rms_scale(ss, ss_res)
            stt("dve", tmpn[:], pm[f][:, :], r, gpost[:], ALU.mult, ALU.mult,
                [("ps", 2 * f), ("ps", 2 * f + 1), r_res, "gpost"], [("rl", 0), ("rl", 1)])
            tt("dve", xb[:, b, :], xb[:, b, :], tmpn[:], ALU.add, [("x", b), ("rl", 0), ("rl", 1)], [("x", b)])
            if tb in (1, 2):
                norm_part(gb0 + tb - 1, ggain, gres)
        transpose_part(gb0 + 0, 0)
        norm_part(gb0 + 2, ggain, gres)
        transpose_part(gb0 + 1, 1)
        norm_part(gb0 + 3, ggain, gres)
        transpose_part(gb0 + 2, 2)
        transpose_part(gb0 + 3, 3)
        ring_release(8)

    def h2gen(i):
        nb0 = (i % 2) * 4
        norm_part(nb0 + 0, gpre2, "gpre2")
        norm_part(nb0 + 1, gpre2, "gpre2")
        transpose_part(nb0 + 0, 0)
        norm_part(nb0 + 2, gpre2, "gpre2")
        transpose_part(nb0 + 1, 1)
        norm_part(nb0 + 3, gpre2, "gpre2")
        transpose_part(nb0 + 2, 2)
        transpose_part(nb0 + 3, 3)

    def mlp(i, nxt_spec):
        xbuf = i % 2
        flush_deferred()
        ckpt("wout")
        for cb in range(4):
            sl = ring_acquire(8)
            for jj in range(8):
                j = cb * 8 + jj
                ps_ap, ps_res = next_half()
                for k in range(8):
                    mm(ps_ap, ring[:, sl[k], jj * 128:(jj + 1) * 128], hT[:, k, :], k == 0, k == 7,
                       HT_ALL + [("ring", sl[k])], [ps_res])
                rb = j % 2
                actf(rl[:, rb, :], ps_ap, AF.Relu, [ps_res], [("rl", rb)])
                eng = "dve" if (j % 2 == 0) else "pool"
                tt(eng, act[:, j, :], rl[:, rb, :], rl[:, rb, :], ALU.mult, [("rl", rb)], [("act", j)])
            ring_release(8)
        ckpt("up")
        ssh = []
        sched = {}
        if nxt_spec is not None:
            nb_, g_, gr_ = nxt_spec
            sched = {
                (0, 1): [lambda: norm_part(nb_ + 0, g_, gr_), lambda: norm_part(nb_ + 1, g_, gr_)],
                (0, 7): [lambda: transpose_part(nb_ + 0, 0), lambda: norm_part(nb_ + 2, g_, gr_)],
                (0, 12): [lambda: transpose_part(nb_ + 1, 1), lambda: norm_part(nb_ + 3, g_, gr_)],
                (1, 2): [lambda: transpose_part(nb_ + 2, 2)],
                (1, 6): [lambda: transpose_part(nb_ + 3, 3)],
            }
        for nh in range(2):
            for jp in range(16):
                sl = ring_acquire(1)
                for jj in range(2):
                    j = jp * 2 + jj
                    for tb in range(4):
                        mm(pm[tb // 2][:, (tb % 2) * 512:(tb % 2 + 1) * 512], act[:, j, tb * 128:(tb + 1) * 128],
                           ring[:, sl[0], jj * 512:(jj + 1) * 512], j == 0, j == 31,
                           [("act", j), ("ring", sl[0])], [("ps", tb)])
                ring_release(1)
                for fn_ in sched.get((nh, jp), []):
                    fn_()
            ckpt("dn_mm%d" % nh)
            if nh == 0:
                for tb in range(4):
                    ps_ap = pm[tb // 2][:, (tb % 2) * 512:(tb % 2 + 1) * 512]
                    copy("act" if tb % 2 == 0 else "dve", ffsb[:, tb, :], ps_ap, [("ps", tb)], [("ffsb", tb)])
                for tb in range(4):
                    ss, ss_res = newstat()
                    actf(rec[:, 0, :], ffsb[:, tb, :], AF.Square, [("ffsb", tb)], [ss_res, ("rec", 0)], accum=ss)
                    ssh.append((ss, ss_res))
                ckpt("dn_ev0")
            else:
                for tb in range(4):
                    b = xbuf * 4 + tb
                    ps_ap = pm[tb // 2][:, (tb % 2) * 512:(tb % 2 + 1) * 512]
                    ss1, ss1_res = newstat()
                    actf(rec[:, 0, :], ps_ap, AF.Square, [("ps", tb)], [ss1_res, ("rec", 0)], accum=ss1)
                    ss0, ss0_res = ssh[tb]
                    sst, sst_res = newstat()
                    tt("dve", sst, ss0, ss1, ALU.add, [ss0_res, ss1_res], [sst_res])
                    r, r_res = rms_scale(sst, sst_res)
                    rb = tb % 2
                    stt("dve", rl[:, rb, :], ps_ap, r, gpost2[:, 512:1024], ALU.mult, ALU.mult,
                        [("ps", tb), r_res, "gpost2"], [("rl", rb)])
                    tt("pool", xb[:, b, 512:1024], xb[:, b, 512:1024], rl[:, rb, :], ALU.add,
                       [("x", b), ("rl", rb)], [("x", b)])

                    def tail_(tb=tb, b=b, r=r, r_res=r_res):
                        stt("dve", ffsb[:, tb, :], ffsb[:, tb, :], r, gpost2[:, 0:512], ALU.mult, ALU.mult,
                            [("ffsb", tb), r_res, "gpost2"], [("ffsb", tb)])
                        tt("pool", xb[:, b, 0:512], xb[:, b, 0:512], ffsb[:, tb, :], ALU.add,
                           [("x", b), ("ffsb", tb)], [("x", b)])
                        store_y(i, tb)
                    if i == ntiles - 1:
                        tail_()
                    else:
                        deferred.append(tail_)
        full_ctr[0] = 2

    try:
        ckpt("pre")
        load_x(0)
        if ntiles >= 2:
            load_x(1)
        front(0)
        ckpt("front")
        if ntiles >= 2:
            mixer(0, gen=(4, gpre, "gpre"))
            issue_casts(range(7, 13))
            mixer(1, gen=(0, gpre2, "gpre2"))
            mlp(0, (4, gpre2, "gpre2"))
            if ntiles > 2:
                load_x(2)
                mlp(1, (0, gpre, "gpre"))
            else:
                mlp(1, None)
            for i in range(2, ntiles):
                mixer(i)
                if i + 1 < ntiles:
                    load_x(i + 1)
                    mlp(i, (((i + 1) % 2) * 4, gpre, "gpre"))
                else:
                    mlp(i, None)
        else:
            issue_casts(range(7, 13))
            mixer(0)
            mlp(0, None)
    except _Stop:
        pass
    flush_deferred()

    counters = None
    tot_pre = 16 * len(pre_ops)
    tot_cst = 16 * len(cst_ops)
    orig_emit = S.emit

    def emit_with_groups():
        cnt = {}
        for op in S.ops:
            if op.is_dma:
                cnt[op.semkey] = cnt.get(op.semkey, 0) + 16 * op.ninc
                op.sigval = cnt[op.semkey]
            elif op.signal:
                cnt[op.semkey] = cnt.get(op.semkey, 0) + 1
                op.sigval = cnt[op.semkey]
        for op in pre_ops:
            op.sigval = tot_pre
        for op in cst_ops:
            op.sigval = tot_cst
        clocks = {k: {} for k in S.engines}
        for op in S.ops:
            E = S.engines[op.eng]
            clk = clocks[op.eng]
            for d in op.deps:
                if clk.get(d.semkey, 0) >= d.sigval:
                    continue
                E.wait_ge(S.sems[d.semkey], d.sigval)
                if d.clock:
                    for k, v in d.clock.items():
                        if clk.get(k, 0) < v:
                            clk[k] = v
                clk[d.semkey] = d.sigval
            res = op.fn()
            if op.is_dma:
                if not isinstance(res, (list, tuple)):
                    res = [res]
                assert len(res) == op.ninc
                for ins in res:
                    ins.then_inc(S.sems[op.semkey], 16)
                op.clock = dict(clk)
            elif op.signal:
                res.then_inc(S.sems[op.semkey], 1)
                op.clock = dict(clk)
        return cnt

    counters = emit_with_groups()
    for b in range(8):
        key = "xs%d" % b
        if counters.get(key, 0) > 0:
            nc.sync.wait_ge(sems[key], counters[key])
    es.close()
    return nc


_CACHE = {}


def kernel(x, g_mix_pre, w_in, b_in, w_pool, pool_scale, attn_sinks, w_branch_pool, w_branch_attn, w_out,
           g_mix_post, g_mlp_pre, w_up, w_down, g_mlp_post):
    f = lambda a: np.ascontiguousarray(np.asarray(a, dtype=np.float32))
    x = f(x)
    shared = {
        "g_mix_pre": f(g_mix_pre)[0:1], "w_in": f(w_in)[0], "b_in": f(b_in)[0:1], "w_pool": f(w_pool)[0],
        "pool_scale": f(pool_scale)[0:1], "attn_sinks": f(attn_sinks)[0:1],
        "w_branch_pool": f(w_branch_pool)[0], "w_branch_attn": f(w_branch_attn)[0], "w_out": f(w_out)[0],
        "g_mix_post": f(g_mix_post)[0:1], "g_mlp_pre": f(g_mlp_pre)[0:1], "w_up": f(w_up)[0],
        "w_down": f(w_down)[0], "g_mlp_post": f(g_mlp_post)[0:1],
    }
    if "nc" not in _CACHE:
        _CACHE["nc"] = build_program()
    nc = _CACHE["nc"]
    in_maps = []
    for c in range(NCORES):
        m = dict(shared)
        m["x"] = np.ascontiguousarray(x[2 * c:2 * c + 2].reshape(TOK_PER_CORE, D))
        in_maps.append(m)
    res = run_bass_kernel_spmd(nc, in_maps, core_ids=list(range(NCORES)))
    out = np.empty((16, SEQ, D), dtype=np.float32)
    for c in range(NCORES):
        out[2 * c:2 * c + 2] = np.asarray(res.results[c]["y"]).reshape(2, SEQ, D)
    return out
```

```python
import contextlib
import os
KNOB = os.environ.get('KNOB', '')
import numpy as np
import concourse.bass as bass
import concourse.mybir as mybir
from concourse.bass_utils import run_bass_kernel_spmd

F32 = mybir.dt.float32
BF16 = mybir.dt.bfloat16
I32 = mybir.dt.int32
AF = mybir.ActivationFunctionType
ALU = mybir.AluOpType

NCORES = 8
D = 1024
SEQ = 2048
INW = 3328
DFF = 4096
TOK_PER_CORE = 2 * SEQ
T = 512
NTILES = TOK_PER_CORE // T
R = 18
EPS = 1e-6
SAFE_DIST = 1 << 30
ROPE_THETA = 500000.0
MAGIC = 12582912.0


class Op:
    __slots__ = ("eng", "fn", "deps", "idx", "signal", "sigval", "semkey", "clock", "is_dma", "ninc")


class Sched:
    def __init__(self, nc, engines, sems):
        self.nc = nc
        self.engines = engines
        self.sems = sems
        self.ops = []
        self.cnt = {k: 0 for k in engines}
        self.last_writer = {}
        self.readers = {}
        self.uid = 0

    def add(self, eng, fn, reads=(), writes=(), dma_sem=None, ninc=1):
        op = Op()
        op.eng = eng
        op.fn = fn
        op.idx = self.cnt[eng]
        self.cnt[eng] += 1
        op.signal = False
        op.is_dma = dma_sem is not None
        op.semkey = dma_sem if op.is_dma else eng
        op.ninc = ninc
        op.sigval = None
        op.clock = None
        deps = {}
        for r in list(reads) + list(writes):
            w = self.last_writer.get(r)
            if w is not None:
                deps[id(w)] = w
        for r in writes:
            for o in self.readers.get(r, {}).values():
                deps[id(o)] = o
        final = []
        for d in deps.values():
            if d is op:
                continue
            if (not d.is_dma) and (not op.is_dma) and d.eng == eng:
                if eng == "pe":
                    continue
                if op.idx - d.idx >= SAFE_DIST:
                    continue
            d.signal = True
            final.append(d)
        op.deps = final
        for r in writes:
            self.last_writer[r] = op
            self.readers[r] = {}
        for r in reads:
            key = eng
            if op.is_dma:
                self.uid += 1
                key = ("dma", self.uid)
            self.readers.setdefault(r, {})[key] = op
        self.ops.append(op)
        return op

    def emit(self):
        counters = {}
        for op in self.ops:
            if op.is_dma:
                counters[op.semkey] = counters.get(op.semkey, 0) + 16 * op.ninc
                op.sigval = counters[op.semkey]
            elif op.signal:
                counters[op.semkey] = counters.get(op.semkey, 0) + 1
                op.sigval = counters[op.semkey]
        clocks = {k: {} for k in self.engines}
        for op in self.ops:
            E = self.engines[op.eng]
            clk = clocks[op.eng]
            for d in op.deps:
                if clk.get(d.semkey, 0) >= d.sigval:
                    continue
                E.wait_ge(self.sems[d.semkey], d.sigval)
                if d.clock:
                    for k, v in d.clock.items():
                        if clk.get(k, 0) < v:
                            clk[k] = v
                clk[d.semkey] = d.sigval
            res = op.fn()
            if op.is_dma:
                if not isinstance(res, (list, tuple)):
                    res = [res]
                assert len(res) == op.ninc
                for ins in res:
                    ins.then_inc(self.sems[op.semkey], 16)
                op.clock = dict(clk)
            elif op.signal:
                res.then_inc(self.sems[op.semkey], 1)
                op.clock = dict(clk)
        return counters


class _Stop(Exception):
    pass


def build_program(ntiles=NTILES, debug=False, stop=""):
    nc = bass.Bass("TRN2", target_bir_lowering=False)
    es = contextlib.ExitStack()

    def din(name, shape):
        return nc.dram_tensor(name, shape, F32, kind="ExternalInput").ap()

    x_d = din("x", [TOK_PER_CORE, D])
    g_mix_pre = din("g_mix_pre", [1, D])
    w_in = din("w_in", [D, INW])
    b_in = din("b_in", [1, INW])
    w_pool = din("w_pool", [4, 128, 128])
    pool_scale = din("pool_scale", [1, 512])
    attn_sinks = din("attn_sinks", [1, 8])
    w_bp = din("w_branch_pool", [512, D])
    w_ba = din("w_branch_attn", [512, D])
    w_out = din("w_out", [D, D])
    g_mix_post = din("g_mix_post", [1, D])
    g_mlp_pre = din("g_mlp_pre", [1, D])
    w_up = din("w_up", [D, DFF])
    w_down = din("w_down", [DFF, D])
    g_mlp_post = din("g_mlp_post", [1, D])
    y_d = nc.dram_tensor("y", [TOK_PER_CORE, D], F32, kind="ExternalOutput").ap()

    def scratch(name, shape):
        return nc.dram_tensor(name, shape, BF16, kind="Internal").ap()

    s_win = scratch("s_win", [D, INW])
    s_bp = scratch("s_bp", [512, D])
    s_ba = scratch("s_ba", [512, D])
    s_wout = scratch("s_wout", [D, D])
    s_wup = scratch("s_wup", [D, DFF])
    s_wdn = scratch("s_wdn", [DFF, D])

    def sb(name, shape, dt):
        return es.enter_context(nc.sbuf_tensor(name, shape, dt))

    def psum(name, shape, dt):
        return es.enter_context(nc.psum_tensor(name, shape, dt))

    ring = sb("ring", [128, R, 1024], BF16)
    xb = sb("xb", [128, 8, 1024], F32)
    hT = sb("hT", [128, 8, 512], BF16)
    htok = sb("htok", [128, 2, 1024], BF16)
    up = sb("up", [128, 4, 528], F32)
    ptmp = sb("ptmp", [128, 2, 528], F32)
    diffT = sb("diffT", [128, 4, 512], BF16)
    act = sb("act", [128, 32, 512], BF16)
    qkvf = sb("qkvf", [128, 2, 768], F32)
    qkb = sb("qkb", [128, 4, 640], BF16)
    ropet = sb("ropet", [128, 2, 4, 80], F32)
    qT = sb("qT", [128, 4, 512], BF16)
    kT = sb("kT", [128, 8, 128], BF16)
    vb = sb("vb", [128, 8, 128], BF16)
    pT = sb("pT", [128, 4, 1024], BF16)
    rec = sb("rec", [128, 1, 512], F32)
    t1 = sb("t1", [128, 2, 512], F32)
    t2 = sb("t2", [128, 2, 512], F32)
    rl = sb("rl", [128, 2, 512], F32)
    tmpn = rl[:, :, :].rearrange("p a b -> p (a b)")
    ffsb = sb("ffsb", [128, 4, 512], F32)
    stats = sb("stats", [128, 128], F32)
    gpre = sb("gpre", [128, 1024], F32)
    gpost = sb("gpost", [128, 1024], F32)
    gpre2 = sb("gpre2", [128, 1024], F32)
    gpost2 = sb("gpost2", [128, 1024], F32)
    bqkv = sb("bqkv", [128, 768], F32)
    bB = sb("bB", [128, 20], F32)
    pscale = sb("pscale", [128, 4], F32)
    sinkexp = sb("sinkexp", [128, 512], F32)
    sink4 = sb("sink4", [128, 4], F32)
    wpool = sb("wpool", [128, 4, 128], BF16)
    ident = sb("ident", [128, 128], BF16)
    maskneg = sb("maskneg", [128, 2, 512], BF16)
    ones = sb("ones", [128, 64], BF16)
    epst = sb("epst", [128, 1], F32)
    posi = sb("posi", [128, 16], I32)
    posf = sb("posf", [128, 16], F32)
    cost = sb("cost", [128, 16, 8], F32)
    sint = sb("sint", [128, 16, 8], F32)
    invc = sb("invc", [128, 4, 16], F32)
    invi = sb("invi", [128, 16], I32)

    identf = tmpn[:, 0:128]
    maskf = tmpn[:, 128:384].rearrange("p (a b) -> p a b", a=2)
    ang = tmpn[:, 384:512].rearrange("p (a b) -> p a b", a=16)
    angk = tmpn[:, 512:640].rearrange("p (a b) -> p a b", a=16)
    angt = tmpn[:, 640:768].rearrange("p (a b) -> p a b", a=16)

    tr = [psum("tr%d" % i, [128, 8, 128], BF16) for i in range(2)]
    pm = [psum("pm%d" % i, [128, 1024], F32) for i in range(3)]

    sems = {}

    def mksem(name):
        sems[name] = es.enter_context(nc.semaphore(name))

    for e in ("pe", "act", "dve", "pool", "pre", "cst"):
        mksem(e)
    for s in range(R):
        mksem("ring%d" % s)
    for g in range(13):
        mksem("cast%d" % g)
    for s in range(R):
        mksem("wb%d" % s)
        mksem("rq%d" % s)
    for b in range(8):
        mksem("xl%d" % b)
        mksem("xs%d" % b)

    engines = {"pe": nc.tensor, "act": nc.scalar, "dve": nc.vector, "pool": nc.gpsimd, "sp": nc.sync}
    S = Sched(nc, engines, sems)
    VE = {"dve": nc.vector, "pool": nc.gpsimd}

    def ckpt(name):
        if stop == name:
            raise _Stop()

    def mm(out, lhsT, rhs, start, stop, reads, writes, tp=None):
        def fn():
            if tp is None:
                return nc.tensor.matmul(out, lhsT=lhsT, rhs=rhs, start=start, stop=stop)
            return nc.tensor.matmul(out, lhsT=lhsT, rhs=rhs, start=start, stop=stop, tile_position=tp)
        S.add("pe", fn, reads, writes)

    def transpose(out, in_, reads, writes):
        S.add("pe", lambda: nc.tensor.transpose(out, in_, ident[:]), list(reads) + ["ident"], writes)

    def actf(out, in_, func, reads, writes, bias=None, scale=None, accum=None):
        def fn():
            kw = {}
            if bias is not None:
                kw["bias"] = bias
            if scale is not None:
                kw["scale"] = scale
            if accum is not None:
                kw["accum_out"] = accum
            return nc.scalar.activation(out=out, in_=in_, func=func, **kw)
        S.add("act", fn, reads, writes)

    def tt(eng, out, in0, in1, op, reads, writes):
        S.add(eng, lambda: VE[eng].tensor_tensor(out=out, in0=in0, in1=in1, op=op), reads, writes)

    def ts(eng, out, in0, s1, s2, op0, op1, reads, writes):
        def fn():
            if op1 is None:
                return VE[eng].tensor_scalar(out=out, in0=in0, scalar1=s1, scalar2=None, op0=op0)
            return VE[eng].tensor_scalar(out=out, in0=in0, scalar1=s1, scalar2=s2, op0=op0, op1=op1)
        S.add(eng, fn, reads, writes)

    def stt(eng, out, in0, scalar, in1, op0, op1, reads, writes):
        S.add(eng, lambda: VE[eng].scalar_tensor_tensor(out=out, in0=in0, scalar=scalar, in1=in1, op0=op0, op1=op1),
              reads, writes)

    def copy(eng, out, in_, reads, writes):
        if eng == "act":
            S.add("act", lambda: nc.scalar.copy(out=out, in_=in_), reads, writes)
        else:
            S.add(eng, lambda: VE[eng].tensor_copy(out=out, in_=in_), reads, writes)

    def recip(out, in_, reads, writes):
        S.add("dve", lambda: nc.vector.reciprocal(out=out, in_=in_), reads, writes)

    def memset(eng, out, val, writes):
        S.add(eng, lambda: VE[eng].memset(out, val), [], writes)

    def dma(eng, out, in_, semkey, reads, writes):
        q = {"sp": nc.sync, "pool": nc.gpsimd}[eng]
        S.add(eng, lambda: q.dma_start(out=out, in_=in_), reads, writes, dma_sem=semkey)

    stat_ctr = [0]

    def newstat():
        c = stat_ctr[0] % 128
        stat_ctr[0] += 1
        return stats[:, c:c + 1], ("st", c)

    cast_groups = {
        0: [(s_win[:, 512:1280], w_in[:, 512:1280])],
        1: [(s_win[:, 0:512], w_in[:, 0:512])],
        2: [(s_win[:, 1280:2304], w_in[:, 1280:2304])],
        3: [(s_win[:, 2304:3328], w_in[:, 2304:3328])],
        4: [(s_bp[:, :], w_bp[:, :])],
        5: [(s_ba[:, :], w_ba[:, :])],
        6: [(s_wout[:, :], w_out[:, :])],
    }
    for cb in range(4):
        cast_groups[7 + cb] = [(s_wup[:, cb * 1024:(cb + 1) * 1024], w_up[:, cb * 1024:(cb + 1) * 1024])]
    for nh in range(2):
        cast_groups[11 + nh] = [(s_wdn[r0:r0 + 1024, nh * 512:(nh + 1) * 512], w_down[r0:r0 + 1024, nh * 512:(nh + 1) * 512])
                                for r0 in range(0, DFF, 1024)]
    NGROUPS = 13

    def issue_casts(groups, after=()):
        for n_, g in enumerate(groups):
            parts = cast_groups[g]

            def fn(parts=parts):
                return [nc.gpsimd.dma_start(out=d_, in_=s_) for (d_, s_) in parts]
            S.add("pool", fn, list(after) if n_ == 0 else [], [("scrg", g)], dma_sem="cast%d" % g, ninc=len(parts))

    pre_ops = []
    op = S.add("pool", lambda: nc.gpsimd.dma_start(out=wpool[:], in_=w_pool.rearrange("g c d -> c g d")),
               [], ["wpool"], dma_sem="pre")
    pre_ops.append(op)

    cst_ops = []

    def cdma(out, in_, w, nonc=False):
        def fn():
            if nonc:
                with nc.allow_non_contiguous_dma(reason="tiny constant load"):
                    return nc.sync.dma_start(out=out, in_=in_)
            return nc.sync.dma_start(out=out, in_=in_)
        cst_ops.append(S.add("sp", fn, [], [w], dma_sem="cst"))

    cdma(gpre[:], g_mix_pre.partition_broadcast(128), "gpre")
    cdma(gpost[:], g_mix_post.partition_broadcast(128), "gpost")
    cdma(gpre2[:], g_mlp_pre.partition_broadcast(128), "gpre2")
    cdma(gpost2[:], g_mlp_post.partition_broadcast(128), "gpost2")
    cdma(bqkv[:], b_in[:, 512:1280].partition_broadcast(128), "bqkv")
    cdma(bB[:, 0:4], b_in[0, 0:512].rearrange("(c p) -> p c", p=128), "bB", nonc=True)
    cdma(bB[:, 4:20], b_in[0, 1280:3328].rearrange("(c p) -> p c", p=128), "bB2", nonc=True)
    cdma(pscale[:], pool_scale[0, :].rearrange("(g p) -> p g", p=128), "pscale", nonc=True)
    cdma(sink4[0:64, :], attn_sinks[:, 0:4].partition_broadcast(64), "sink4a")
    cdma(sink4[64:128, :], attn_sinks[:, 4:8].partition_broadcast(64), "sink4b")

    memset("pool", identf[:], 1.0, ["identf"])
    S.add("pool", lambda: nc.gpsimd.affine_select(out=identf[:], in_=identf[:], pattern=[[-1, 128]],
                                                  compare_op=ALU.is_equal, fill=0.0, base=0, channel_multiplier=1),
          ["identf"], ["identf"])
    copy("dve", ident[:], identf[:], ["identf"], ["ident"])
    memset("pool", maskf[:], 1.0, ["maskf"])
    S.add("pool", lambda: nc.gpsimd.affine_select(out=maskf[:, 0, :], in_=maskf[:, 0, :], pattern=[[-1, 128]],
                                                  compare_op=ALU.is_gt, fill=0.0, base=0, channel_multiplier=1),
          ["maskf"], ["maskf"])
    S.add("pool", lambda: nc.gpsimd.affine_select(out=maskf[:, 1, :], in_=maskf[:, 1, :], pattern=[[1, 128]],
                                                  compare_op=ALU.is_ge, fill=0.0, base=0, channel_multiplier=-1),
          ["maskf"], ["maskf"])
    for kb_ in range(2):
        ts("dve", maskneg[:, kb_, :].rearrange("p (g q) -> p g q", g=4),
           maskf[:, kb_, :].unsqueeze(1).broadcast_to([128, 4, 128]), -1.0, 30000.0, ALU.add, ALU.mult,
           ["maskf"], ["maskneg"])
    memset("dve", ones[:], 1.0, ["ones"])
    memset("dve", epst[:], EPS, ["eps"])
    actf(sink4[:], sink4[:], AF.Exp, ["sink4a", "sink4b"], ["sink4"])
    copy("dve", sinkexp[:].rearrange("p (g q) -> p g q", g=4), sink4[:].unsqueeze(2).broadcast_to([128, 4, 128]),
         ["sink4"], ["sinkexp"])
    S.add("pool", lambda: nc.gpsimd.iota(posi[:], pattern=[[128, 16]], base=0, channel_multiplier=1), [], ["posi"])
    copy("dve", posf[:], posi[:], ["posi"], ["posf"])
    for i in range(8):
        inv_freq = float(np.float32(ROPE_THETA) ** np.float32(-(2.0 * i) / 16.0))
        ts("dve", ang[:, :, i], posf[:], inv_freq, None, ALU.mult, None, ["posf"], ["ang"])
    C1 = 6.28125
    C2 = float(2.0 * np.pi - 6.28125)
    for (tab, shift, nm) in ((sint, 0.0, "sint"), (cost, float(np.pi / 2), "cost")):
        ts("dve", angt[:], ang[:], shift, None, ALU.add, None, ["ang"], ["angt"])
        ts("dve", angk[:], angt[:], float(1.0 / (2.0 * np.pi)), MAGIC, ALU.mult, ALU.add, ["angt"], ["angk"])
        ts("dve", angk[:], angk[:], -MAGIC, None, ALU.add, None, ["angk"], ["angk"])
        stt("dve", angt[:], angk[:], -C1, angt[:], ALU.mult, ALU.add, ["angk", "angt"], ["angt"])
        stt("dve", angt[:], angk[:], -C2, angt[:], ALU.mult, ALU.add, ["angk", "angt"], ["angt"])
        ts("dve", angt[:], angt[:], 3.1415925, -3.1415925, ALU.min, ALU.max, ["angt"], ["angt"])
        actf(tab[:], angt[:], AF.Sin, ["angt"], [nm])
    S.add("pool", lambda: nc.gpsimd.iota(invi[:], pattern=[[1, 16]], base=1, channel_multiplier=0), [], ["invi"])
    for gi in range(4):
        copy("dve", invc[:, gi, :], invi[:], ["invi"], ["invc"])
    for gi in range(4):
        ts("dve", invc[:, gi, :], invc[:, gi, :], float(2 ** (gi + 1)), None, ALU.min, None, ["invc"], ["invc"])
    recip(invc[:], invc[:], ["invc"], ["invc"])

    CONST_R = ["gpre", "gpost", "gpre2", "gpost2", "bqkv", "bB", "bB2", "pscale", "sinkexp", "wpool", "ident",
               "maskneg", "ones", "eps", "sint", "cost", "invc"]

    pieces = []

    def both(slicer, s_t, w_t):
        return slicer(s_t), slicer(w_t)

    def mixer_parts():
        out = []
        for k in range(8):
            out.append([(lambda s: ring[:, s, 0:768],) + both(lambda t, k=k: t[k * 128:(k + 1) * 128, 512:1280], s_win, w_in)])
        for k in range(8):
            out.append([(lambda s: ring[:, s, 0:512],) + both(lambda t, k=k: t[k * 128:(k + 1) * 128, 0:512], s_win, w_in)])
        for cb in range(2):
            for k in range(8):
                out.append([(lambda s: ring[:, s, :],) + both(
                    lambda t, k=k, cb=cb: t[k * 128:(k + 1) * 128, 1280 + cb * 1024:1280 + (cb + 1) * 1024], s_win, w_in)])
        for c in range(4):
            out.append([(lambda s: ring[:, s, :],) + both(lambda t, c=c: t[c * 128:(c + 1) * 128, :], s_bp, w_bp)])
        for g in range(4):
            out.append([(lambda s: ring[0:64, s, :],) + both(lambda t, g=g: t[g * 64:(g + 1) * 64, :], s_ba, w_ba),
                        (lambda s: ring[64:128, s, :],) + both(lambda t, g=g: t[(4 + g) * 64:(5 + g) * 64, :], s_ba, w_ba)])
        for k in range(8):
            out.append([(lambda s: ring[:, s, :],) + both(lambda t, k=k: t[k * 128:(k + 1) * 128, :], s_wout, w_out)])
        return out

    def mlp_parts():
        out = []
        for cb in range(4):
            for k in range(8):
                out.append((7 + cb, [(lambda s: ring[:, s, :],) + both(
                    lambda t, k=k, cb=cb: t[k * 128:(k + 1) * 128, cb * 1024:(cb + 1) * 1024], s_wup, w_up)]))
        for nh in range(2):
            for jp in range(16):
                out.append((11 + nh, [(lambda s: ring[:, s, :].rearrange("p (two n) -> p two n", two=2),) + both(
                    lambda t, jp=jp, nh=nh: t[jp * 256:(jp + 1) * 256, nh * 512:(nh + 1) * 512].rearrange("(two p) n -> p two n", p=128),
                    s_wdn, w_down)]))
        return out

    def M(first):
        return [("sw", ("scr", j), p) if first else ("hw", ("scr", j), p) for j, p in enumerate(mixer_parts())]

    def L():
        return [("hw", ("scrg", g), p) for (g, p) in mlp_parts()]

    if ntiles >= 2:
        pieces += M(True) + M(False) + L() + L()
        for i in range(2, ntiles):
            pieces += M(False) + L()
    else:
        pieces += M(True) + L()
    rstate = {"loaded": 0, "cp": 0}

    def ring_prefetch(upto):
        upto = min(upto, len(pieces) - 1)
        while rstate["loaded"] <= upto:
            m = rstate["loaded"]
            s = m % R
            mode, res, parts = pieces[m]
            if mode == "sw":
                def fn(parts=parts, s=s):
                    return [nc.gpsimd.dma_start(out=dst(s), in_=w32) for (dst, sc, w32) in parts]
                S.add("pool", fn, [], [("ring", s)], dma_sem="rq%d" % s, ninc=len(parts))

                def fnw(parts=parts, s=s):
                    return [nc.sync.dma_start(out=sc, in_=dst(s)) for (dst, sc, w32) in parts]
                S.add("sp", fnw, [("ring", s)], [res], dma_sem="wb%d" % s, ninc=len(parts))
            else:
                def fn(parts=parts, s=s):
                    return [nc.sync.dma_start(out=dst(s), in_=sc) for (dst, sc, w32) in parts]
                S.add("sp", fn, [res], [("ring", s)], dma_sem="ring%d" % s, ninc=len(parts))
            rstate["loaded"] += 1

    def ring_acquire(n):
        cp = rstate["cp"]
        ring_prefetch(cp + R - 1)
        assert rstate["loaded"] >= cp + n
        return [(cp + i) % R for i in range(n)]

    def ring_release(n):
        rstate["cp"] += n
        ring_prefetch(rstate["cp"] + R - 1)

    deferred = []

    def flush_deferred():
        while deferred:
            deferred.pop(0)()

    def load_x(i):
        flush_deferred()
        xbuf = i % 2
        for tb in range(4):
            b = xbuf * 4 + tb
            r0 = i * T + tb * 128
            dma("sp", xb[:, b, :], x_d[r0:r0 + 128, :], "xl%d" % b, [], [("x", b)])

    def store_y(i, tb):
        b = (i % 2) * 4 + tb
        r0 = i * T + tb * 128
        dma("sp", y_d[r0:r0 + 128, :], xb[:, b, :], "xs%d" % b, [("x", b)], [("y", i, tb)])

    bank_ctr = [0]

    def next_half():
        h = bank_ctr[0] % 6
        bank_ctr[0] += 1
        return pm[h // 2][:, (h % 2) * 512:(h % 2 + 1) * 512], ("ps", h)

    full_ctr = [0]

    def next_full():
        f = full_ctr[0] % 3
        full_ctr[0] += 1
        return f

    tr_ctr = [0]

    def next_tr():
        t_ = tr_ctr[0] % 2
        tr_ctr[0] += 1
        return tr[t_], ("tr", t_)

    def rms_scale(ss_ap, ss_res):
        rs, rs_res = newstat()
        actf(rs, ss_ap, AF.Sqrt, [ss_res, "eps"], [rs_res], bias=epst[:], scale=1.0 / D)
        r, r_res = newstat()
        recip(r, rs, [rs_res], [r_res])
        return r, r_res

    def norm_transpose(b, tb, gain, gain_res):
        hb = b % 2
        ss, ss_res = newstat()
        actf(htok[:, hb, :], xb[:, b, :], AF.Square, [("x", b)], [ss_res, ("htok", hb)], accum=ss)
        r, r_res = rms_scale(ss, ss_res)
        stt("dve", htok[:, hb, :], xb[:, b, :], r, gain[:], ALU.mult, ALU.mult,
            [("x", b), r_res, gain_res], [("htok", hb)])
        trt, tr_res = next_tr()
        for k in range(8):
            transpose(trt[:, k, :], htok[:, hb, k * 128:(k + 1) * 128], [("htok", hb)], [tr_res])
        copy("act", hT[:, :, tb * 128:(tb + 1) * 128], trt[:, :, :], [tr_res], [("hT", tb)])

    def front(i):
        for tb in range(4):
            b = (i % 2) * 4 + tb
            norm_transpose(b, tb, gpre, "gpre")

    HT_ALL = [("hT", tb) for tb in range(4)]

    def norm_part(b, gain, gain_res):
        hb = b % 2
        ss, ss_res = newstat()
        actf(htok[:, hb, :], xb[:, b, :], AF.Square, [("x", b)], [ss_res, ("htok", hb)], accum=ss)
        r, r_res = rms_scale(ss, ss_res)
        stt("dve", htok[:, hb, :], xb[:, b, :], r, gain[:], ALU.mult, ALU.mult,
            [("x", b), r_res, gain_res], [("htok", hb)])

    def transpose_pe(b):
        hb = b % 2
        trt, tr_res = next_tr()
        for k in range(8):
            transpose(trt[:, k, :], htok[:, hb, k * 128:(k + 1) * 128], [("htok", hb)], [tr_res])
        return trt, tr_res

    def transpose_evac(tb, trt, tr_res):
        copy("act", hT[:, :, tb * 128:(tb + 1) * 128], trt[:, :, :], [tr_res], [("hT", tb)])

    def transpose_part(b, tb):
        transpose_evac(tb, *transpose_pe(b))

    def gate_block(cb):
        sl = ring_acquire(8)
        for cc in range(8):
            c = cb * 8 + cc
            ps_ap, ps_res = next_half()
            for k in range(8):
                mm(ps_ap, ring[:, sl[k], cc * 128:(cc + 1) * 128], hT[:, k, :], k == 0, k == 7,
                   HT_ALL + [("ring", sl[k])], [ps_res])
            actf(act[:, c, :], ps_ap, AF.Sigmoid, [ps_res, "bB2"], [("act", c)], bias=bB[:, 4 + c:5 + c])
        ring_release(8)

    def mixer(i, gen="own"):
        xbuf = i % 2
        if gen == "own":
            gb0, ggain, gres = xbuf * 4, gpre2, "gpre2"
        else:
            gb0, ggain, gres = gen
        ti = i % 4
        sl = ring_acquire(8)
        for tb in range(4):
            f = next_full()
            for k in range(8):
                lhsT = hT[:, k, tb * 128:(tb + 1) * 128]
                mm(pm[f][:, 0:512], lhsT, ring[:, sl[k], 0:512], k == 0, k == 7,
                   [("hT", tb), ("ring", sl[k])], [("ps", 2 * f)])
                mm(pm[f][:, 512:768], lhsT, ring[:, sl[k], 512:768], k == 0, k == 7,
                   [("hT", tb), ("ring", sl[k])], [("ps", 2 * f + 1)])
            qb = tb % 2
            tt("dve", qkvf[:, qb, 0:512].rearrange("p (g kv d) -> p kv g d", g=4, kv=2),
               pm[f][:, 0:512].rearrange("p (kv g d) -> p kv g d", kv=2, g=4),
               bqkv[:, 0:512].rearrange("p (kv g d) -> p kv g d", kv=2, g=4), ALU.add,
               [("ps", 2 * f), "bqkv"], [("qkvf", qb, 0)])
            tt("dve", qkvf[:, qb, 512:768], pm[f][:, 512:768], bqkv[:, 512:768], ALU.add,
               [("ps", 2 * f + 1), "bqkv"], [("qkvf", qb, 1)])
            gb = ti * 4 + tb
            qk = qkvf[:, qb, 0:640].rearrange("p (h d) -> p h d", h=10)
            ob = qkb[:, tb, :].rearrange("p (h d) -> p h d", h=10)
            cb_ = cost[:, gb, :].unsqueeze(1).broadcast_to([128, 10, 8])
            sb_ = sint[:, gb, :].unsqueeze(1).broadcast_to([128, 10, 8])
            tA = ropet[:, qb, 0, :].rearrange("p (h d) -> p h d", h=10)
            tB = ropet[:, qb, 1, :].rearrange("p (h d) -> p h d", h=10)
            tC = ropet[:, qb, 2, :].rearrange("p (h d) -> p h d", h=10)
            tD = ropet[:, qb, 3, :].rearrange("p (h d) -> p h d", h=10)
            qres = [("qkvf", qb, 0), ("qkvf", qb, 1)]
            tt("pool", tA, qk[:, :, 0:8], cb_, ALU.mult, qres + ["cost"], [("ropet", qb, 0)])
            tt("pool", tB, qk[:, :, 8:16], sb_, ALU.mult, qres + ["sint"], [("ropet", qb, 1)])
            tt("dve", tC, qk[:, :, 8:16], cb_, ALU.mult, qres + ["cost"], [("ropet", qb, 2)])
            tt("dve", tD, qk[:, :, 0:8], sb_, ALU.mult, qres + ["sint"], [("ropet", qb, 3)])
            copy("act", ob[:, :, 16:64], qk[:, :, 16:64], qres, [("qkb", tb, 2)])
            tt("pool", ob[:, :, 0:8], tA, tB, ALU.subtract, [("ropet", qb, 0), ("ropet", qb, 1)], [("qkb", tb, 0)])
            tt("dve", ob[:, :, 8:16], tC, tD, ALU.add, [("ropet", qb, 2), ("ropet", qb, 3)], [("qkb", tb, 1)])
            slot = gb % 8
            copy("pool", vb[:, slot, :], qkvf[:, qb, 640:768], [("qkvf", qb, 1)], [("vb", slot)])
        ring_release(8)
        flush_deferred()
        ckpt("qkv")

        sl = ring_acquire(8)
        for c in range(4):
            ps_ap, ps_res = next_half()
            for k in range(8):
                mm(ps_ap, ring[:, sl[k], c * 128:(c + 1) * 128], hT[:, k, :], k == 0, k == 7,
                   HT_ALL + [("ring", sl[k])], [ps_res])
            actf(up[:, c, 16:528], ps_ap, AF.Identity, [ps_res, "bB"], [("up", c)], bias=bB[:, c:c + 1])
        ring_release(8)
        gate_block(0)

        for tb in range(4):
            gb = ti * 4 + tb
            slot = gb % 8
            trt, tr_res = next_tr()
            qkb_res = [("qkb", tb, 0), ("qkb", tb, 1), ("qkb", tb, 2)]
            for g in range(4):
                transpose(trt[:, g, :], qkb[:, tb, g * 128:(g + 1) * 128], qkb_res, [tr_res])
            transpose(trt[:, 4, :], qkb[:, tb, 512:640], qkb_res, [tr_res])
            copy("act", qT[:, tb, :].rearrange("p (g q) -> p g q", g=4), trt[:, 0:4, :], [tr_res], [("qT", tb)])
            copy("act", kT[:, slot, :], trt[:, 4, :], [tr_res], [("kT", slot)])

        if ti == 0:
            memset("pool", up[:, :, 0:16], 0.0, [("uph", g) for g in range(4)])
        for gi in range(4):
            w = 2 ** (gi + 1)
            src = up[:, gi, :]
            src_res = [("up", gi), ("uph", gi)]
            cur, cur_res = src, src_res
            sh = 1
            lo = 1
            for st_ in range(gi + 1):
                dst = ptmp[:, st_ % 2, :]
                dres = ("ptmp", st_ % 2)
                tt("dve", dst[:, lo:528], cur[:, lo:528], cur[:, lo - sh:528 - sh], ALU.add, cur_res, [dres])
                cur, cur_res = dst, [dres]
                sh *= 2
                lo += sh
            tmpc = ptmp[:, (gi + 1) % 2, 0:15]
            tres = ("ptmp", (gi + 1) % 2)
            stt("dve", diffT[:, gi, :], cur[:, 16:528], 1.0 / w, src[:, 16:528], ALU.mult, ALU.subtract,
                cur_res + src_res, [("diffT", gi)])
            if ti == 0:
                tt("dve", tmpc, cur[:, 16:31], invc[:, gi, 0:15], ALU.mult, cur_res + ["invc"], [tres])
                tt("dve", diffT[:, gi, 0:15], tmpc, src[:, 16:31], ALU.subtract, [tres] + src_res, [("diffT", gi)])
            if ti != 3:
                copy("pool", up[:, gi, 0:16], up[:, gi, 512:528], [("up", gi)], [("uph", gi)])

        gate_block(1)
        ckpt("gates")

        for gi in range(4):
            ps_ap, ps_res = next_half()
            mm(ps_ap, wpool[:, gi, :], diffT[:, gi, :], True, True, ["wpool", ("diffT", gi)], [ps_res])
            ts("dve", act[:, 28 + gi, :], ps_ap, pscale[:, gi:gi + 1], None, ALU.mult, None,
               [ps_res, "pscale"], [("act", 28 + gi)])

        ckpt("pool")
        def att_scores(tb):
            gb = ti * 4 + tb
            slot = gb % 8
            pslot = (gb - 1) % 8
            kbs = ([(0, pslot)] if gb > 0 else []) + [(1, slot)]
            for kv in range(2):
                pbuf = (tb % 2) * 2 + kv
                pv = pT[:, pbuf, :].rearrange("p (kb n) -> p kb n", kb=2)
                for (kb, ks) in kbs:
                    mm(pm[kv][:, kb * 512:(kb + 1) * 512], kT[kv * 64:(kv + 1) * 64, ks, :],
                       qT[kv * 64:(kv + 1) * 64, tb, :], True, False,
                       [("kT", ks), ("qT", tb)], [("ps", 2 * kv + kb)], tp=(kv * 64, 0))
                    mm(pm[kv][:, kb * 512:(kb + 1) * 512], ident[:, :], maskneg[:, kb, :], False, True,
                       ["ident", "maskneg"], [("ps", 2 * kv + kb)])
                    actf(pv[:, kb, :], pm[kv][:, kb * 512:(kb + 1) * 512], AF.Exp, [("ps", 2 * kv + kb)],
                         [("pT", pbuf, kb)], scale=0.125)

        def att_pv(tb):
            gb = ti * 4 + tb
            slot = gb % 8
            pslot = (gb - 1) % 8
            kbs = ([(0, pslot)] if gb > 0 else []) + [(1, slot)]
            n = len(kbs)
            if tb % 2 == 0:
                o_ps, o_res = pm[2][:, 0:512], ("ps", 4)
                d_ps, d_res = pm[2][:, 512:1024], ("ps", 5)
                rc, rc_res = rec[:, 0, :], ("rec", 0)
            else:
                o_ps, o_res = tr[0][:].rearrange("p a b -> p (a b)").bitcast(F32), ("tr", 0)
                d_ps, d_res = tr[1][:].rearrange("p a b -> p (a b)").bitcast(F32), ("tr", 1)
                rc, rc_res = ptmp[:, 0, 0:512], ("ptmp", 0)
            for kv in range(2):
                pbuf = (tb % 2) * 2 + kv
                pv = pT[:, pbuf, :].rearrange("p (kb n) -> p kb n", kb=2)
                for idx, (kb, ks) in enumerate(kbs):
                    mm(o_ps[kv * 64:(kv + 1) * 64, :], vb[:, ks, kv * 64:(kv + 1) * 64], pv[:, kb, :],
                       idx == 0, idx == n - 1, [("vb", ks), ("pT", pbuf, kb)], [o_res], tp=(0, kv * 64))
                for idx, (kb, ks) in enumerate(kbs):
                    mm(d_ps[kv * 64:(kv + 1) * 64, :], ones[:, :], pv[:, kb, :],
                       idx == 0, idx == n - 1, ["ones", ("pT", pbuf, kb)], [d_res], tp=(0, kv * 64))
            tt("dve", rc, d_ps, sinkexp[:], ALU.add, [d_res, "sinkexp"], [rc_res])
            actf(rc, rc, AF.Ln, [rc_res], [rc_res])
            actf(rc, rc, AF.Exp, [rc_res], [rc_res], scale=-1.0)
            tt("dve", act[:, 24:28, tb * 128:(tb + 1) * 128], o_ps.rearrange("p (g q) -> p g q", g=4),
               rc.rearrange("p (g q) -> p g q", g=4), ALU.mult,
               [o_res, rc_res], [("act", 24 + g) for g in range(4)])

        att_scores(0)
        for tb in range(4):
            if tb + 1 < 4:
                att_scores(tb + 1)
            att_pv(tb)

        ckpt("attn")
        sl = ring_acquire(8)
        for fo in range(8):
            f = next_full()
            for c in range(4):
                mm(pm[f][:, 0:512], ring[:, sl[c], fo * 128:(fo + 1) * 128], act[:, 28 + c, :], c == 0, c == 3,
                   [("ring", sl[c]), ("act", 28 + c)], [("ps", 2 * f)])
            for g in range(4):
                mm(pm[f][:, 512:1024], ring[:, sl[4 + g], fo * 128:(fo + 1) * 128], act[:, 24 + g, :], g == 0, g == 3,
                   [("ring", sl[4 + g]), ("act", 24 + g)], [("ps", 2 * f + 1)])
            tbuf = fo % 2
            tt("dve", t1[:, tbuf, :], pm[f][:, 0:512], act[:, fo, :], ALU.mult,
               [("ps", 2 * f), ("act", fo)], [("t1", tbuf)])
            tt("dve", t2[:, tbuf, :], pm[f][:, 512:1024], act[:, 8 + fo, :], ALU.mult,
               [("ps", 2 * f + 1), ("act", 8 + fo)], [("t2", tbuf)])
            tt("pool", act[:, 16 + fo, :], t1[:, tbuf, :], t2[:, tbuf, :], ALU.add,
               [("t1", tbuf), ("t2", tbuf)], [("act", 16 + fo)])
        ring_release(8)

        ckpt("branch")
        sl = ring_acquire(8)

        for tb in range(4):
            b = xbuf * 4 + tb
            f = next_full()
            for k in range(8):
                lhsT = act[:, 16 + k, tb * 128:(tb + 1) * 128]
                mm(pm[f][:, 0:512], lhsT, ring[:, sl[k], 0:512], k == 0, k == 7,
                   [("act", 16 + k), ("ring", sl[k])], [("ps", 2 * f)])
                mm(pm[f][:, 512:1024], lhsT, ring[:, sl[k], 512:1024], k == 0, k == 7,
                   [("act", 16 + k), ("ring", sl[k])], [("ps", 2 * f + 1)])
            ss, ss_res = newstat()
            actf(ptmp[:, :, :].rearrange("p a b -> p (a b)")[:, 0:1024], pm[f][:, :], AF.Square,
                 [("ps", 2 * f), ("ps", 2 * f + 1)], [ss_res, ("ptmp", 0), ("ptmp", 1)], accum=ss)
            r, r_res = rms_scale(ss, ss_res)
            stt("dve", tmpn[:], pm[f][:, :], r, gpost[:], ALU.mult, ALU.mult,
                [("ps", 2 * f), ("ps", 2 * f + 1), r_res, "gpost"], [("rl", 0), ("rl", 1)])
            tt("dve", xb[:, b, :], xb[:, b, :], tmpn[:], ALU.add, [("x", b), ("rl", 0), ("rl", 1)], [("x", b)])
            if tb in (1, 2):
                norm_part(gb0 + tb - 1, ggain, gres)
        t0_ = transpose_pe(gb0 + 0)
        norm_part(gb0 + 2, ggain, gres)
        t1_ = transpose_pe(gb0 + 1)
        norm_part(gb0 + 3, ggain, gres)
        transpose_evac(0, *t0_)
        transpose_evac(1, *t1_)
        transpose_part(gb0 + 2, 2)
        transpose_part(gb0 + 3, 3)
        ring_release(8)

    def h2gen(i):
        nb0 = (i % 2) * 4
        norm_part(nb0 + 0, gpre2, "gpre2")
        norm_part(nb0 + 1, gpre2, "gpre2")
        transpose_part(nb0 + 0, 0)
        norm_part(nb0 + 2, gpre2, "gpre2")
        transpose_part(nb0 + 1, 1)
        norm_part(nb0 + 3, gpre2, "gpre2")
        transpose_part(nb0 + 2, 2)
        transpose_part(nb0 + 3, 3)

    def mlp(i, nxt_spec):
        xbuf = i % 2
        flush_deferred()
        ckpt("wout")
        for cb in range(4):
            sl = ring_acquire(8)
            for jj in range(8):
                j = cb * 8 + jj
                ps_ap, ps_res = next_half()
                for k in range(8):
                    mm(ps_ap, ring[:, sl[k], jj * 128:(jj + 1) * 128], hT[:, k, :], k == 0, k == 7,
                       HT_ALL + [("ring", sl[k])], [ps_res])
                rb = j % 2
                actf(rl[:, rb, :], ps_ap, AF.Relu, [ps_res], [("rl", rb)])
                eng = "dve" if (j % 2 == 0) else "pool"
                tt(eng, act[:, j, :], rl[:, rb, :], rl[:, rb, :], ALU.mult, [("rl", rb)], [("act", j)])
            ring_release(8)
        ckpt("up")
        ssh = []
        sched = {}
        if nxt_spec is not None:
            nb_, g_, gr_ = nxt_spec
            sched = {
                (0, 1): [lambda: norm_part(nb_ + 0, g_, gr_), lambda: norm_part(nb_ + 1, g_, gr_)],
                (0, 7): [lambda: transpose_part(nb_ + 0, 0), lambda: norm_part(nb_ + 2, g_, gr_)],
                (0, 12): [lambda: transpose_part(nb_ + 1, 1), lambda: norm_part(nb_ + 3, g_, gr_)],
                (1, 2): [lambda: transpose_part(nb_ + 2, 2)],
                (1, 6): [lambda: transpose_part(nb_ + 3, 3)],
            }
        for nh in range(2):
            for jp in range(16):
                sl = ring_acquire(1)
                for jj in range(2):
                    j = jp * 2 + jj
                    for tb in range(4):
                        mm(pm[tb // 2][:, (tb % 2) * 512:(tb % 2 + 1) * 512], act[:, j, tb * 128:(tb + 1) * 128],
                           ring[:, sl[0], jj * 512:(jj + 1) * 512], j == 0, j == 31,
                           [("act", j), ("ring", sl[0])], [("ps", tb)])
                ring_release(1)
                for fn_ in sched.get((nh, jp), []):
                    fn_()
            ckpt("dn_mm%d" % nh)
            if nh == 0:
                for tb in range(4):
                    ps_ap = pm[tb // 2][:, (tb % 2) * 512:(tb % 2 + 1) * 512]
                    copy("act" if tb % 2 == 0 else "dve", ffsb[:, tb, :], ps_ap, [("ps", tb)], [("ffsb", tb)])
                for tb in range(4):
                    ss, ss_res = newstat()
                    actf(rec[:, 0, :], ffsb[:, tb, :], AF.Square, [("ffsb", tb)], [ss_res, ("rec", 0)], accum=ss)
                    ssh.append((ss, ss_res))
                ckpt("dn_ev0")
            else:
                for tb in range(4):
                    b = xbuf * 4 + tb
                    ps_ap = pm[tb // 2][:, (tb % 2) * 512:(tb % 2 + 1) * 512]
                    ss1, ss1_res = newstat()
                    actf(rec[:, 0, :], ps_ap, AF.Square, [("ps", tb)], [ss1_res, ("rec", 0)], accum=ss1)
                    ss0, ss0_res = ssh[tb]
                    sst, sst_res = newstat()
                    tt("dve", sst, ss0, ss1, ALU.add, [ss0_res, ss1_res], [sst_res])
                    r, r_res = rms_scale(sst, sst_res)
                    rb = tb % 2
                    stt("dve", rl[:, rb, :], ps_ap, r, gpost2[:, 512:1024], ALU.mult, ALU.mult,
                        [("ps", tb), r_res, "gpost2"], [("rl", rb)])
                    tt("pool", xb[:, b, 512:1024], xb[:, b, 512:1024], rl[:, rb, :], ALU.add,
                       [("x", b), ("rl", rb)], [("x", b)])

                    def tail_(tb=tb, b=b, r=r, r_res=r_res):
                        stt("dve", ffsb[:, tb, :], ffsb[:, tb, :], r, gpost2[:, 0:512], ALU.mult, ALU.mult,
                            [("ffsb", tb), r_res, "gpost2"], [("ffsb", tb)])
                        tt("pool", xb[:, b, 0:512], xb[:, b, 0:512], ffsb[:, tb, :], ALU.add,
                           [("x", b), ("ffsb", tb)], [("x", b)])
                        store_y(i, tb)
                    if i == ntiles - 1:
                        tail_()
                    else:
                        deferred.append(tail_)
        full_ctr[0] = 2

    try:
        ckpt("pre")
        load_x(0)
        if ntiles >= 2:
            load_x(1)
        front(0)
        ckpt("front")
        if ntiles >= 2:
            mixer(0, gen=(4, gpre, "gpre"))
            issue_casts(range(7, 13))
            mixer(1, gen=(0, gpre2, "gpre2"))
            mlp(0, (4, gpre2, "gpre2"))
            if ntiles > 2:
                load_x(2)
                mlp(1, (0, gpre, "gpre"))
            else:
                mlp(1, None)
            for i in range(2, ntiles):
                mixer(i)
                if i + 1 < ntiles:
                    load_x(i + 1)
                    mlp(i, (((i + 1) % 2) * 4, gpre, "gpre"))
                else:
                    mlp(i, None)
        else:
            issue_casts(range(7, 13))
            mixer(0)
            mlp(0, None)
    except _Stop:
        pass
    flush_deferred()

    counters = None
    tot_pre = 16 * len(pre_ops)
    tot_cst = 16 * len(cst_ops)
    orig_emit = S.emit

    def emit_with_groups():
        cnt = {}
        for op in S.ops:
            if op.is_dma:
                cnt[op.semkey] = cnt.get(op.semkey, 0) + 16 * op.ninc
                op.sigval = cnt[op.semkey]
            elif op.signal:
                cnt[op.semkey] = cnt.get(op.semkey, 0) + 1
                op.sigval = cnt[op.semkey]
        for op in pre_ops:
            op.sigval = tot_pre
        for op in cst_ops:
            op.sigval = tot_cst
        clocks = {k: {} for k in S.engines}
        for op in S.ops:
            E = S.engines[op.eng]
            clk = clocks[op.eng]
            for d in op.deps:
                if clk.get(d.semkey, 0) >= d.sigval:
                    continue
                E.wait_ge(S.sems[d.semkey], d.sigval)
                if d.clock:
                    for k, v in d.clock.items():
                        if clk.get(k, 0) < v:
                            clk[k] = v
                clk[d.semkey] = d.sigval
            res = op.fn()
            if op.is_dma:
                if not isinstance(res, (list, tuple)):
                    res = [res]
                assert len(res) == op.ninc
                for ins in res:
                    ins.then_inc(S.sems[op.semkey], 16)
                op.clock = dict(clk)
            elif op.signal:
                res.then_inc(S.sems[op.semkey], 1)
                op.clock = dict(clk)
        return cnt

    counters = emit_with_groups()
    for b in range(8):
        key = "xs%d" % b
        if counters.get(key, 0) > 0:
            nc.sync.wait_ge(sems[key], counters[key])
    es.close()
    return nc


_CACHE = {}


def kernel(x, g_mix_pre, w_in, b_in, w_pool, pool_scale, attn_sinks, w_branch_pool, w_branch_attn, w_out,
           g_mix_post, g_mlp_pre, w_up, w_down, g_mlp_post):
    f = lambda a: np.ascontiguousarray(np.asarray(a, dtype=np.float32))
    x = f(x)
    shared = {
        "g_mix_pre": f(g_mix_pre)[0:1], "w_in": f(w_in)[0], "b_in": f(b_in)[0:1], "w_pool": f(w_pool)[0],
        "pool_scale": f(pool_scale)[0:1], "attn_sinks": f(attn_sinks)[0:1],
        "w_branch_pool": f(w_branch_pool)[0], "w_branch_attn": f(w_branch_attn)[0], "w_out": f(w_out)[0],
        "g_mix_post": f(g_mix_post)[0:1], "g_mlp_pre": f(g_mlp_pre)[0:1], "w_up": f(w_up)[0],
        "w_down": f(w_down)[0], "g_mlp_post": f(g_mlp_post)[0:1],
    }
    if "nc" not in _CACHE:
        _CACHE["nc"] = build_program()
    nc = _CACHE["nc"]
    in_maps = []
    for c in range(NCORES):
        m = dict(shared)
        m["x"] = np.ascontiguousarray(x[2 * c:2 * c + 2].reshape(TOK_PER_CORE, D))
        in_maps.append(m)
    res = run_bass_kernel_spmd(nc, in_maps, core_ids=list(range(NCORES)))
    out = np.empty((16, SEQ, D), dtype=np.float32)
    for c in range(NCORES):
        out[2 * c:2 * c + 2] = np.asarray(res.results[c]["y"]).reshape(2, SEQ, D)
    return out
```
